# Optimizing a Trainium2 kernel written in Bass

```python
import jax, jax.numpy as jnp
from jax import lax
import numpy as np

D_MODEL = 1024
BATCH = 8
SEQ = 8192
DEPTH = 4
DEC_BATCH = 32
DEC_SEQ = 16
PAST_LEN = 1024

CHUNK = 64
N_HEADS = 16
HEAD_DIM = D_MODEL // N_HEADS
BAND_CHUNKS = 8
BAND_PAST = BAND_CHUNKS * CHUNK
BAND_LEN = BAND_PAST + CHUNK
MAX_REL = 128
CONV_WIDTH = 31
D_FF = ((8 * D_MODEL // 3 + 127) // 128) * 128
PE_DIM = 256
N_MIXERS = 2
N_CONV = (DEPTH + 1) // 2
N_ATTN = DEPTH // 2
EPS = 1e-6
NEG_INF = -1e30

kernel_name = 'streaming_conformer_hybrid_step'


def rmsnorm(x, g):
    x32 = x.astype(jnp.float32)
    y = x32 * lax.rsqrt(jnp.mean(x32 * x32, axis=-1, keepdims=True) + EPS)
    return (y * g.astype(jnp.float32)).astype(x.dtype)


def swiglu(h, w_in, w_out):
    a, b = jnp.split(h @ w_in, 2, axis=-1)
    return (jax.nn.silu(a) * b) @ w_out


def conv_module(h, buf, w_in, b_in, dw, dw_b, g, w_out, b_out):
    a, gate = jnp.split(h @ w_in + b_in, 2, axis=-1)
    glu = a * jax.nn.sigmoid(gate)
    padded = jnp.concatenate([buf.astype(glu.dtype), glu], axis=1)
    y = lax.conv_general_dilated(padded, dw[:, None, :].astype(glu.dtype), (1,), 'VALID',
                                 dimension_numbers=('NWC', 'WIO', 'NWC'),
                                 feature_group_count=D_MODEL) + dw_b
    y = jax.nn.silu(rmsnorm(y, g))
    return y @ w_out + b_out, padded[:, padded.shape[1] - (CONV_WIDTH - 1):]


def rel_bias(table, qpos, kpos):
    rel = jnp.clip(qpos[:, None] - kpos[None, :], -MAX_REL, MAX_REL) + MAX_REL
    return table[:, rel]


def band_mask(qpos, kpos):
    qc = qpos // CHUNK
    kc = kpos // CHUNK
    return ((kpos[None, :] >= 0) & (kc[None, :] <= qc[:, None])
            & (kc[None, :] >= qc[:, None] - BAND_CHUNKS))


def band_attend(q, k, v, bias, mask):
    s = jnp.einsum('bqhd,bkhd->bhqk', q, k).astype(jnp.float32) * (HEAD_DIM ** -0.5)
    s = jnp.where(mask, s + bias.astype(jnp.float32), NEG_INF)
    p = jax.nn.softmax(s, axis=-1).astype(v.dtype)
    return jnp.einsum('bhqk,bkhd->bqhd', p, v)


def attn_prompt(h, w_qkv, w_o, table):
    B, S, _ = h.shape
    nc = S // CHUNK
    qkv = (h @ w_qkv).reshape(B, S, 3, N_HEADS, HEAD_DIM)
    q, k, v = qkv[:, :, 0], qkv[:, :, 1], qkv[:, :, 2]
    pad = jnp.zeros((B, BAND_PAST, N_HEADS, HEAD_DIM), k.dtype)
    kp = jnp.concatenate([pad, k], axis=1)
    vp = jnp.concatenate([pad, v], axis=1)
    r = jnp.arange(CHUNK)
    kk = jnp.arange(BAND_LEN) - BAND_PAST
    bias = rel_bias(table, r, kk)
    q_chunks = q.reshape(B, nc, CHUNK, N_HEADS, HEAD_DIM).transpose(1, 0, 2, 3, 4)

    def one_chunk(args):
        c, q_blk = args
        start = c * CHUNK
        k_blk = lax.dynamic_slice_in_dim(kp, start, BAND_LEN, axis=1)
        v_blk = lax.dynamic_slice_in_dim(vp, start, BAND_LEN, axis=1)
        return band_attend(q_blk, k_blk, v_blk, bias, band_mask(start + r, start + kk))

    o = lax.map(one_chunk, (jnp.arange(nc), q_chunks))
    o = o.transpose(1, 0, 2, 3, 4).reshape(B, S, D_MODEL)
    keep = min(BAND_PAST, S)
    return o @ w_o, k[:, S - keep:], v[:, S - keep:]


def attn_sample(h, ck, cv, w_qkv, w_o, table):
    B, T, _ = h.shape
    W = ck.shape[1]
    qkv = (h @ w_qkv).reshape(B, T, 3, N_HEADS, HEAD_DIM)
    q, k, v = qkv[:, :, 0], qkv[:, :, 1], qkv[:, :, 2]
    k_all = jnp.concatenate([ck.astype(k.dtype), k], axis=1)
    v_all = jnp.concatenate([cv.astype(v.dtype), v], axis=1)
    qpos = PAST_LEN + jnp.arange(T)
    kpos = jnp.concatenate([PAST_LEN - W + jnp.arange(W), qpos])
    o = band_attend(q, k_all, v_all, rel_bias(table, qpos, kpos), band_mask(qpos, kpos))
    return o.reshape(B, T, D_MODEL) @ w_o, k, v


def trunk(x, p, conv_state, cache_k, cache_v, params):
    (norm_g, final_g, ffn_w_in, ffn_w_out, conv_w_in, conv_b_in, conv_dw, conv_dw_b, conv_norm_g,
     conv_w_out, conv_b_out, attn_w_qkv, attn_w_o, attn_rel_table, pe_w_proj, pe_w_gate) = params
    is_prompt = conv_state is None
    new_conv, new_k, new_v = [], [], []
    for i in range(DEPTH):
        g = norm_g[i]
        x = x + 0.5 * swiglu(rmsnorm(x, g[0]), ffn_w_in[i, 0], ffn_w_out[i, 0])
        h = rmsnorm(x, g[1])
        j = i // N_MIXERS
        if i % N_MIXERS == 0:
            buf = jnp.zeros((x.shape[0], CONV_WIDTH - 1, D_MODEL), x.dtype) if is_prompt else conv_state[j]
            y, nb = conv_module(h, buf, conv_w_in[j], conv_b_in[j], conv_dw[j], conv_dw_b[j],
                                conv_norm_g[j], conv_w_out[j], conv_b_out[j])
            new_conv.append(nb)
        else:
            if is_prompt:
                y, nk, nv = attn_prompt(h, attn_w_qkv[j], attn_w_o[j], attn_rel_table[j])
            else:
                y, nk, nv = attn_sample(h, cache_k[j], cache_v[j], attn_w_qkv[j], attn_w_o[j],
                                        attn_rel_table[j])
            new_k.append(nk)
            new_v.append(nv)
        x = x + y
        x = x + 0.5 * swiglu(rmsnorm(x, g[2]), ffn_w_in[i, 1], ffn_w_out[i, 1])
        gate = jax.nn.sigmoid(rmsnorm(x, g[3]) @ pe_w_gate[i])
        x = x + gate * (p[i] @ pe_w_proj[i])
    return rmsnorm(x, final_g), jnp.stack(new_conv), jnp.stack(new_k), jnp.stack(new_v)


def setup_inputs(seed: int = 0) -> dict:
    key = jax.random.key(seed)
    ks = jax.random.split(key, 24)
    nrm = jax.random.normal
    f32 = jnp.float32
    kv_win = min(BAND_PAST, PAST_LEN)
    return {
        'x_prompt': nrm(ks[0], (BATCH, SEQ, D_MODEL), f32),
        'x_sample': nrm(ks[1], (DEC_BATCH, DEC_SEQ, D_MODEL), f32),
        'cache_conv': 0.5 * nrm(ks[2], (N_CONV, DEC_BATCH, CONV_WIDTH - 1, D_MODEL), f32),
        'cache_k': nrm(ks[3], (N_ATTN, DEC_BATCH, kv_win, N_HEADS, HEAD_DIM), f32),
        'cache_v': nrm(ks[4], (N_ATTN, DEC_BATCH, kv_win, N_HEADS, HEAD_DIM), f32),
        'p_prompt': nrm(ks[5], (DEPTH, BATCH, SEQ, PE_DIM), f32),
        'p_sample': nrm(ks[6], (DEPTH, DEC_BATCH, DEC_SEQ, PE_DIM), f32),
        'norm_g': 1.0 + 0.01 * nrm(ks[7], (DEPTH, 4, D_MODEL), f32),
        'final_g': 1.0 + 0.01 * nrm(ks[8], (D_MODEL,), f32),
        'ffn_w_in': nrm(ks[9], (DEPTH, 2, D_MODEL, 2 * D_FF), f32) * D_MODEL ** -0.5,
        'ffn_w_out': nrm(ks[10], (DEPTH, 2, D_FF, D_MODEL), f32) * D_FF ** -0.5,
        'conv_w_in': nrm(ks[11], (N_CONV, D_MODEL, 2 * D_MODEL), f32) * D_MODEL ** -0.5,
        'conv_b_in': 0.01 * nrm(ks[12], (N_CONV, 2 * D_MODEL), f32),
        'conv_dw': nrm(ks[13], (N_CONV, CONV_WIDTH, D_MODEL), f32) * CONV_WIDTH ** -0.5,
        'conv_dw_b': 0.01 * nrm(ks[14], (N_CONV, D_MODEL), f32),
        'conv_norm_g': 1.0 + 0.01 * nrm(ks[15], (N_CONV, D_MODEL), f32),
        'conv_w_out': nrm(ks[16], (N_CONV, D_MODEL, D_MODEL), f32) * D_MODEL ** -0.5,
        'conv_b_out': 0.01 * nrm(ks[17], (N_CONV, D_MODEL), f32),
        'attn_w_qkv': nrm(ks[18], (N_ATTN, D_MODEL, 3 * D_MODEL), f32) * D_MODEL ** -0.5,
        'attn_w_o': nrm(ks[19], (N_ATTN, D_MODEL, D_MODEL), f32) * D_MODEL ** -0.5,
        'attn_rel_table': 0.1 * nrm(ks[20], (N_ATTN, N_HEADS, 2 * MAX_REL + 1), f32),
        'pe_w_proj': nrm(ks[21], (DEPTH, PE_DIM, D_MODEL), f32) * PE_DIM ** -0.5,
        'pe_w_gate': nrm(ks[22], (DEPTH, D_MODEL, D_MODEL), f32) * D_MODEL ** -0.5,
    }


def reference(x_prompt, x_sample, cache_conv, cache_k, cache_v, p_prompt, p_sample,
              norm_g, final_g, ffn_w_in, ffn_w_out, conv_w_in, conv_b_in, conv_dw, conv_dw_b,
              conv_norm_g, conv_w_out, conv_b_out, attn_w_qkv, attn_w_o, attn_rel_table,
              pe_w_proj, pe_w_gate):
    params = (norm_g, final_g, ffn_w_in, ffn_w_out, conv_w_in, conv_b_in, conv_dw, conv_dw_b,
              conv_norm_g, conv_w_out, conv_b_out, attn_w_qkv, attn_w_o, attn_rel_table,
              pe_w_proj, pe_w_gate)
    y_prompt, conv_prompt, k_prompt, v_prompt = trunk(x_prompt, p_prompt, None, None, None, params)
    y_sample, conv_sample, k_sample, v_sample = trunk(x_sample, p_sample, cache_conv, cache_k, cache_v,
                                                      params)
    return (y_prompt, y_sample, conv_prompt, k_prompt, v_prompt, conv_sample, k_sample, v_sample)
```

```python
import numpy as np
from contextlib import ExitStack
import concourse.bass as bass
import concourse.mybir as mybir
from concourse.bass_utils import run_bass_kernel_spmd

F32 = mybir.dt.float32
BF16 = mybir.dt.bfloat16
AF = mybir.ActivationFunctionType
ALU = mybir.AluOpType

ENGS = ["tensor", "vector", "scalar", "gpsimd", "sync"]

D = 1024
DC = 8
DFF = 2816
FC = 22
T = 512
NH = 16
HD = 64
DEPTH = 4
CW = 31
PED = 256
NSLOT = 5
NCV = 8
CVLOOK = 16
EPS = 1e-6
NSEQ_S = 4
LS = 16
NS = NSEQ_S * LS
KVW = 512


class Buf:
    __slots__ = ("name", "w", "r", "excl")

    def __init__(self, name, excl=False):
        self.name = name
        self.w = None
        self.r = []
        self.excl = excl


class DSem:
    __slots__ = ("sem", "count", "key")

    def __init__(self, sem, key):
        self.sem = sem
        self.count = 0
        self.key = key


class Sched:
    def __init__(self, nc, eng_sems):
        self.nc = nc
        self.q = {e: [] for e in ENGS}
        self.esem = eng_sems
        self.cnt = {e: 0 for e in ENGS}
        self.seen = {e: {} for e in ENGS}
        self.semobj = dict(eng_sems)
        self.dsems = []

    def new_dsem(self, sem, name):
        d = DSem(sem, "d:" + name)
        self.semobj[d.key] = sem
        self.dsems.append(d)
        return d

    def _waits_for(self, eng, reads, writes):
        need = {}

        def add(tok):
            if tok is None:
                return
            k, v = tok
            if need.get(k, 0) < v:
                need[k] = v

        for b in reads:
            add(b.w)
            if b.excl:
                for t in b.r:
                    if t[0] != eng:
                        add(t)
        for b in writes:
            add(b.w)
            for t in b.r:
                add(t)
        out = []
        seen = self.seen[eng]
        for k, v in need.items():
            if k == "tensor" and eng == "tensor":
                continue
            if seen.get(k, 0) >= v:
                continue
            seen[k] = v
            out.append((k, v))
        return out

    def _record(self, tok, reads, writes):
        for b in reads:
            if len(b.r) > 24:
                best = {}
                for k, v in b.r:
                    if best.get(k, 0) < v:
                        best[k] = v
                b.r = list(best.items())
            b.r.append(tok)
        for b in writes:
            b.w = tok
            b.r = []

    def task(self, eng, fns, reads=(), writes=()):
        waits = self._waits_for(eng, reads, writes)
        self.cnt[eng] += 1
        tok = (eng, self.cnt[eng])
        self._record(tok, reads, writes)
        self.q[eng].append((waits, fns, None))
        return tok

    def dma(self, eng, fns, dsem, reads=(), writes=()):
        waits = self._waits_for(eng, reads, writes)
        dsem.count += 16 * len(fns)
        tok = (dsem.key, dsem.count)
        self._record(tok, reads, writes)
        self.q[eng].append((waits, fns, dsem))
        return tok

    def _all_tokens(self):
        toks = [(e, self.cnt[e]) for e in ENGS if self.cnt[e] > 0]
        toks += [(d.key, d.count) for d in self.dsems if d.count > 0]
        return toks

    def barrier(self):
        toks = self._all_tokens()
        for e in ENGS:
            waits = []
            for k, v in toks:
                if k == e and e == "tensor":
                    continue
                if self.seen[e].get(k, 0) >= v:
                    continue
                self.seen[e][k] = v
                waits.append((k, v))
            if waits:
                self.q[e].append((waits, [], None))

    def final_wait(self, eng="sync"):
        self.q[eng].append((list(self._all_tokens()), [], None))

    def emit(self, block):
        def run(engname, e):
            own = self.esem.get(engname)
            for waits, fns, dsem in self.q[engname]:
                for k, v in waits:
                    e.wait_ge(self.semobj[k], v)
                n = len(fns)
                for i, fn in enumerate(fns):
                    ins = fn(e)
                    if dsem is not None:
                        ins.then_inc(dsem.sem, 16)
                    elif i == n - 1:
                        ins.then_inc(own, 1)

        @block.tensor
        def _(e):
            run("tensor", e)

        @block.vector
        def _(e):
            run("vector", e)

        @block.scalar
        def _(e):
            run("scalar", e)

        @block.gpsimd
        def _(e):
            run("gpsimd", e)

        @block.sync
        def _(e):
            run("sync", e)


def MM(out, lhsT, rhs, start, stop):
    return lambda e: e.matmul(out=out, lhsT=lhsT, rhs=rhs, start=start, stop=stop)


def TR(out, in_, ident):
    return lambda e: e.transpose(out=out, in_=in_, identity=ident)


def ACT(out, in_, func, bias=None, scale=None):
    kw = {}
    if bias is not None:
        kw["bias"] = bias
    if scale is not None:
        kw["scale"] = scale
    return lambda e: e.activation(out=out, in_=in_, func=func, **kw)


def TT(out, in0, in1, op):
    return lambda e: e.tensor_tensor(out=out, in0=in0, in1=in1, op=op)


def STT(out, in0, scalar, in1, op0, op1):
    return lambda e: e.scalar_tensor_tensor(out=out, in0=in0, scalar=scalar, in1=in1, op0=op0, op1=op1)


def TS(out, in0, scalar1, op0, scalar2=None, op1=None):
    if op1 is None:
        return lambda e: e.tensor_scalar(out=out, in0=in0, scalar1=scalar1, scalar2=None, op0=op0)
    return lambda e: e.tensor_scalar(out=out, in0=in0, scalar1=scalar1, scalar2=scalar2, op0=op0, op1=op1)


def CP(out, in_):
    return lambda e: e.tensor_copy(out=out, in_=in_)


def RCP(out, in_):
    return lambda e: e.reciprocal(out=out, in_=in_)


def MS(ap, v):
    return lambda e: e.memset(ap, v)


def DMA(out, in_):
    return lambda e: e.dma_start(out=out, in_=in_)


R_NORM = 0
R_FINAL = 16
R_CBIN = 17
R_DWB = 21
R_CNG = 23
R_CBOUT = 25
R_DW = 27
NVROW = 27 + 2 * CW


class Builder:
    def __init__(self, SEQ):
        assert SEQ % T == 0 and SEQ >= T
        self.SEQ = SEQ
        self.NT = SEQ // T
        self.nc = bass.Bass("TRN2", target_bir_lowering=False)
        self.es = ExitStack()

    def din(self, name, shape, dt=F32):
        return self.nc.dram_tensor(name, list(shape), dt, kind="ExternalInput").ap()

    def dout(self, name, shape, dt=F32):
        return self.nc.dram_tensor(name, list(shape), dt, kind="ExternalOutput").ap()

    def dscr(self, name, shape, dt):
        return self.nc.dram_tensor(name, list(shape), dt, kind="Internal").ap()

    def sb(self, name, shape, dt):
        return self.es.enter_context(self.nc.sbuf_tensor(name, list(shape), dt))

    def sem(self, name):
        return self.es.enter_context(self.nc.semaphore(name))

    def dsem(self, name):
        return self.S.new_dsem(self.sem("d_" + name), name)

    def build(self):
        nc = self.nc
        SEQ, NT = self.SEQ, self.NT
        with self.es:
            self._declare_dram()
            self._alloc()
            self._pieces()
            self._prologue()
            self.S.barrier()
            self._main()
            self.S.final_wait("sync")
            with nc.Block() as block:
                self.S.emit(block)
        return nc

    def _declare_dram(self):
        SEQ = self.SEQ
        I = self.I = {}
        I["xp"] = self.din("xp", [SEQ, D])
        I["xs"] = self.din("xs", [NS, D])
        I["cconv"] = self.din("cconv", [2 * NSEQ_S * 30, D])
        I["ck"] = self.din("ck", [2 * NSEQ_S * KVW, D])
        I["cv"] = self.din("cv", [2 * NSEQ_S * KVW, D])
        I["pp"] = self.din("pp", [DEPTH * SEQ, PED])
        I["psm"] = self.din("psm", [DEPTH * NS, PED])
        I["norm_g"] = self.din("norm_g", [16, D])
        I["final_g"] = self.din("final_g", [1, D])
        I["ffn_w_in"] = self.din("ffn_w_in", [8 * D, 2 * DFF])
        I["ffn_w_out"] = self.din("ffn_w_out", [8 * DFF, D])
        I["conv_w_in"] = self.din("conv_w_in", [2 * D, 2 * D])
        I["conv_b_in"] = self.din("conv_b_in", [4, D])
        I["conv_dw"] = self.din("conv_dw", [2 * CW, D])
        I["conv_dw_b"] = self.din("conv_dw_b", [2, D])
        I["conv_norm_g"] = self.din("conv_norm_g", [2, D])
        I["conv_w_out"] = self.din("conv_w_out", [2 * D, D])
        I["conv_b_out"] = self.din("conv_b_out", [2, D])
        I["attn_w_qkv"] = self.din("attn_w_qkv", [2 * D, 3 * D])
        I["attn_w_o"] = self.din("attn_w_o", [2 * D, D])
        I["rel_table"] = self.din("rel_table", [2 * NH, 257])
        I["pe_w_proj"] = self.din("pe_w_proj", [DEPTH * PED, D])
        I["pe_w_gate"] = self.din("pe_w_gate", [DEPTH * D, D])
        I["ident"] = self.din("ident", [128, 128])
        O = self.O = {}
        O["yp"] = self.dout("yp", [SEQ, D])
        O["ys"] = self.dout("ys", [NS, D])
        O["convp"] = self.dout("convp", [2 * 30, D])
        O["kp"] = self.dout("kp", [2 * KVW, D])
        O["vp"] = self.dout("vp", [2 * KVW, D])
        O["convs"] = self.dout("convs", [2 * NSEQ_S * 30, D])
        O["ks"] = self.dout("ks", [2 * NS, D])
        O["vs"] = self.dout("vs", [2 * NS, D])

    def _alloc(self):
        nc = self.nc
        sb = self.sb
        esems = {e: self.sem("s_" + e) for e in ENGS}
        self.S = Sched(nc, esems)
        self.X = sb("X", [128, DC, T], F32)
        self.XB = [Buf("X%d" % c) for c in range(DC)]
        self.XN = sb("XN", [128, DC, T], BF16)
        self.XNB = [Buf("XN%d" % c) for c in range(DC)]
        self.RS = sb("RS", [128, T], F32)
        self.RSB = Buf("RS")
        self.SQ = sb("SQ", [128, DC, T], BF16)
        self.SQB = [Buf("SQ%d" % c) for c in range(DC)]
        self.st_q = []
        self.st_n = 0
        self.fresh_norm = False
        self.SA = sb("SA", [128, 2, T], F32)
        self.SAB = [Buf("SA0"), Buf("SA1")]
        self.sa_i = 0
        self.U = sb("U", [128, 6656], F32)
        self.UB = self.U.bitcast(BF16)
        self.KT = sb("KT", [128, DC, 2 * T], BF16)
        self.KTprevB = Buf("KTprev")
        self.KTcurB = [Buf("KTcur%d" % c) for c in range(DC)]
        self.V = sb("V", [128, 8, D], BF16)
        self.VprevB = Buf("Vprev")
        self.VcurB = [Buf("Vcur%d" % t) for t in range(4)]
        self.EB = sb("EB", [128, NH, 5, 128], BF16)
        self.EBB = Buf("EB")
        self.KST = sb("KST", [128, 4, D], BF16)
        self.KSTB = Buf("KST")
        self.WS = [sb("WS%d" % s, [128, 4096], BF16) for s in range(NSLOT)]
        self.WB = [Buf("WS%d" % s) for s in range(NSLOT)]
        self.XIN = sb("XIN", [128, 2, D], F32)
        self.XINB = [Buf("XIN0"), Buf("XIN1")]
        self.YOUT = sb("YOUT", [128, 2, D], F32)
        self.YOUTB = [Buf("YOUT0"), Buf("YOUT1")]
        self.yo_i = 0
        self.PIN = sb("PIN", [128, 4, PED], F32)
        self.PINB = Buf("PIN")
        self.PTT = sb("PTT", [128, 2, T], BF16)
        self.PTTB = Buf("PTT")
        self.identf = sb("identf", [128, 128], F32)
        self.identb = sb("identb", [128, 128], BF16)
        self.onesm = sb("onesm", [128, 128], BF16)
        self.ones64 = sb("ones64", [128, 64], BF16)
        self.CB = Buf("consts")
        self.vecs = sb("vecs", [128, DC, 96], F32)
        self.VCB = Buf("vecs")
        self.PFX = [sb("PFX%d" % j, [128, DC, 30], BF16) for j in range(2)]
        self.PFXB = [Buf("PFX0"), Buf("PFX1")]
        self.O32 = sb("O32", [128, DC, NSEQ_S, 30], F32)
        self.O32B = Buf("O32")
        self.RD = sb("RD", [128, 2, 128], F32)
        self.RDB = [Buf("RD0"), Buf("RD1")]
        self.rd_i = 0
        self.ps = self.es.enter_context(nc.psum_tensor("ps", [128, 4096], F32))
        self.psb = self.ps.bitcast(BF16)
        self.PB = [Buf("PS%d" % b, excl=True) for b in range(8)]
        self.bk = 7
        self.wdsem = [self.dsem("w%d" % s) for s in range(NSLOT)]
        self.d_xin = [self.dsem("xin0"), self.dsem("xin1")]
        self.d_yout = [self.dsem("yout0"), self.dsem("yout1")]
        self.d_pin = self.dsem("pin")
        self.d_misc = self.dsem("misc")
        self.miscB = Buf("miscchain")
        self.d_eb = self.dsem("eb")
        self.d_kvl = self.dsem("kvl")
        self.d_kvs = self.dsem("kvs")
        self.d_kst = self.dsem("kst")
        self.d_o32 = self.dsem("o32")
        self.rtab = self.dscr("rtab", [2 * NH, 385], F32)
        self.rtabB = Buf("rtab")
        self.ebsc = self.dscr("ebsc", [2 * 128, NH * 5 * 128], BF16)
        self.ebscB = [Buf("ebsc0"), Buf("ebsc1")]
        self.kvsc = self.dscr("kvsc", [4 * 128, 4096], BF16)
        self.kvscB = [Buf("kvsc%d" % i) for i in range(4)]

    def bank(self):
        self.bk = (self.bk + 1) % 7
        return self.bk

    def stat_add(self, src, srcB, c, N):
        self.st_q.append([src, srcB, c, N, 0])

    def _stat_sq(self, it):
        src, srcB, c, N, _ = it
        self.S.task("scalar", [ACT(self.SQ[:, c, 0:N], src[:, c, 0:N], AF.Square)], reads=[srcB[c]], writes=[self.SQB[c]])
        it[4] = 1

    def _stat_mm(self, it):
        src, srcB, c, N, _ = it
        assert c == self.st_n
        self.S.task("tensor", [MM(self.ps[:, 7 * 512:7 * 512 + N], self.onesm[:], self.SQ[:, c, 0:N], c == 0, c == DC - 1)],
                    reads=[self.SQB[c], self.CB], writes=[self.PB[7]])
        self.st_n += 1
        it[4] = 2

    def stat_tick(self, drain=False):
        for it in self.st_q:
            if it[4] == 1:
                self._stat_mm(it)
        for it in self.st_q:
            if it[4] == 0:
                self._stat_sq(it)
        if drain:
            for it in self.st_q:
                if it[4] == 1:
                    self._stat_mm(it)
        self.st_q = [it for it in self.st_q if it[4] != 2]

    def mdma(self, fns, reads=(), writes=()):
        return self.S.dma("sync", fns, self.d_misc, reads=list(reads), writes=list(writes) + [self.miscB])

    def pbank(self, b, n=T):
        return self.ps[:, b * 512:b * 512 + n]

    def vrow(self, c, r):
        return self.vecs[:, c, r:r + 1]

    def _pieces(self):
        I = self.I
        P = self.pieces = []
        order = self.order = []

        def add(kind, args, plen):
            P.append((kind, args, plen))
            order.append(len(P) - 1)

        for i in range(DEPTH):
            j = i // 2

            def ffn(k):
                r0 = (i * 2 + k) * D
                for p in range(11):
                    add("cols", (I["ffn_w_in"], r0, 2 * DFF, 512,
                                 [(p * 256, 256, 0), (DFF + p * 256, 256, 256)]), 4096)
                for m in range(DC):
                    add("wout", (I["ffn_w_out"], (i * 2 + k) * DFF, m), FC * 128)

            ffn(0)
            if i % 2 == 0:
                for q in range(4):
                    add("cols", (I["conv_w_in"], j * D, 2 * D, 512,
                                 [(q * 256, 256, 0), (D + q * 256, 256, 256)]), 4096)
                for m in range(DC):
                    add("dw", (j, m), CW * 128)
                for q in range(2):
                    add("cols", (I["conv_w_out"], j * D, D, 512, [(q * 512, 512, 0)]), 4096)
            else:
                for q in range(6):
                    add("cols", (I["attn_w_qkv"], j * D, 3 * D, 512, [(q * 512, 512, 0)]), 4096)
                for q in range(2):
                    add("cols", (I["attn_w_o"], j * D, D, 512, [(q * 512, 512, 0)]), 4096)
            ffn(1)
            add("proj", (i,), 2 * D)
            for q in range(2):
                add("cols", (I["pe_w_gate"], i * D, D, 512, [(q * 512, 512, 0)]), 4096)
        self.npieces = len(P)
        self.wsc = self.dscr("wsc", [self.npieces * 128, 4096], BF16)
        self.wscPB = [Buf("wsc%d" % p) for p in range(self.npieces)]
        self.d_cv = [self.dsem("cv%d" % k) for k in range(NCV)]
        self.cvchain = [Buf("cvchain%d" % k) for k in range(NCV)]
        self.cv_next = 0

    def convert_piece(self, pid, prologue=False):
        S = self.S
        kind, args, plen = self.pieces[pid]
        dstrows = self.wsc[pid * 128:(pid + 1) * 128, :]
        k = pid % NCV
        if kind == "dw":
            if not prologue:
                return
            j, m = args
            stg = self.UB[:, 0:CW * 128].rearrange("p (k x) -> p k x", k=CW)
            fns = [TS(stg[:, kk, :], self.identb[:], self.vrow(m, R_DW + j * CW + kk), ALU.mult) for kk in range(CW)]
            S.task("vector", fns, reads=[self.CB, self.VCB], writes=[self.stgB])
            self.mdma([DMA(dstrows[:, 0:CW * 128], self.UB[:, 0:CW * 128])], reads=[self.stgB], writes=[self.wscPB[pid]])
            return
        if prologue:
            return
        if kind == "cols":
            w2d, r0, ld, pw, segs = args
            dst = dstrows[:, 0:DC * pw].rearrange("p (c x) -> p c x", c=DC)
            fns = []
            for (c0, w, doff) in segs:
                src = bass.AP(tensor=w2d.tensor, offset=r0 * ld + c0, ap=[[ld, 128], [128 * ld, DC], [1, w]])
                fns.append(DMA(dst[:, :, doff:doff + w], src))
        elif kind == "wout":
            w2d, r0, m = args
            dst = dstrows[:, 0:FC * 128].rearrange("p (c x) -> p c x", c=FC)
            src = bass.AP(tensor=w2d.tensor, offset=r0 * D + m * 128, ap=[[D, 128], [128 * D, FC], [1, 128]])
            fns = [DMA(dst, src)]
        else:
            (i,) = args
            w2d = self.I["pe_w_proj"]
            dst = dstrows[:, 0:2 * D].rearrange("p (c x) -> p c x", c=2)
            src = bass.AP(tensor=w2d.tensor, offset=i * PED * D, ap=[[D, 128], [128 * D, 2], [1, D]])
            fns = [DMA(dst, src)]
        S.dma("gpsimd", fns, self.d_cv[k], writes=[self.wscPB[pid], self.cvchain[k]])

    def _prologue(self):
        S = self.S
        I = self.I
        self.mdma([DMA(self.identf[:], I["ident"])], writes=[self.CB])
        S.task("vector", [CP(self.identb[:], self.identf[:]), MS(self.onesm[:], 1.0 / D), MS(self.ones64[:], 1.0),
                          MS(self.PFX[0][:], 0.0), MS(self.PFX[1][:], 0.0)],
               reads=[self.CB], writes=[self.CB, self.PFXB[0], self.PFXB[1]])
        VST = self.U[0:NVROW, 0:D]
        vstB = Buf("vst")
        srcs = [("norm_g", R_NORM, 16), ("final_g", R_FINAL, 1), ("conv_b_in", R_CBIN, 4), ("conv_dw_b", R_DWB, 2),
                ("conv_norm_g", R_CNG, 2), ("conv_b_out", R_CBOUT, 2), ("conv_dw", R_DW, 2 * CW)]
        self.mdma([DMA(self.U[r0:r0 + n, 0:D], I[nm]) for nm, r0, n in srcs], writes=[vstB])
        for half in range(2):
            b = self.bank()
            S.task("tensor", [TR(self.ps[:, b * 512 + cc * 96:b * 512 + cc * 96 + NVROW],
                                 VST[:, (half * 4 + cc) * 128:(half * 4 + cc + 1) * 128], self.identf[0:NVROW, 0:NVROW])
                              for cc in range(4)], reads=[vstB, self.CB], writes=[self.PB[b]])
            S.task("vector", [CP(self.vecs[:, half * 4:half * 4 + 4, 0:NVROW],
                                 self.ps[:, b * 512:b * 512 + 384].rearrange("p (c r) -> p c r", c=4)[:, :, 0:NVROW])],
                   reads=[self.PB[b]], writes=[self.VCB])
        self.stgB = Buf("stg")
        S.task("vector", [MS(self.RS[:, 0:1], 0.0)], reads=[self.VCB], writes=[vstB, self.stgB])
        for pid in range(self.npieces):
            self.convert_piece(pid, prologue=True)
        for j in range(2):
            self._build_eb(j)

    def _build_eb(self, j):
        S = self.S
        I = self.I
        tabx = self.U[0:NH, 0:385]
        etab = self.U[0:NH, 400:785]
        rt = self.U[0:NH, 800:1185]
        onesr = self.U[0:NH, 1200:1328]
        HK = self.U[:, 2048:2048 + NH * 128].rearrange("p (h q) -> p h q", h=NH)
        tB = Buf("tabx")
        hkB = Buf("hk")
        self.mdma([DMA(tabx[:, 0:257], I["rel_table"][j * NH:(j + 1) * NH, :])], writes=[tB, self.stgB])
        S.task("vector", [MS(onesr, 1.0), TS(tabx[:, 257:385], onesr, tabx[:, 256:257], ALU.mult)],
               reads=[tB], writes=[tB])
        S.task("scalar", [ACT(etab, tabx, AF.Exp)], reads=[tB], writes=[tB])
        rev = bass.AP(tensor=self.U.tensor if hasattr(self.U, "tensor") else self.U,
                      offset=etab.offset + 384, ap=[list(etab.ap[0]), [-1, 385]])
        S.task("vector", [CP(rt, rev)], reads=[tB], writes=[tB])
        self.mdma([DMA(self.rtab[j * NH:(j + 1) * NH, :], rt)], reads=[tB], writes=[self.rtabB])
        for r in range(5):
            if r == 4:
                off, pst = 129, 1
            elif r == 3:
                off, pst = 1, 1
            else:
                off, pst = 0, 0
            src = bass.AP(tensor=self.rtab.tensor, offset=j * NH * 385 + off, ap=[[pst, 128], [385, NH], [1, 128]])
            self.mdma([DMA(HK, src)], reads=[self.rtabB], writes=[hkB])
            hkr = bass.AP(tensor=HK.tensor, offset=HK.offset + 127, ap=[list(HK.ap[0]), [128, NH], [-1, 128]])
            S.task("vector", [CP(self.EB[:, :, r, :], hkr)], reads=[hkB], writes=[self.EBB])
        S.task("vector", [MS(self.EB[0:64, :, 0, 64:128], 0.0), MS(self.EB[64:128, :, 4, 0:64], 0.0)],
               reads=[], writes=[self.EBB])
        self.mdma([DMA(self.ebsc[j * 128:(j + 1) * 128, :], self.EB[:].rearrange("p h r q -> p (h r q)"))],
                  reads=[self.EBB], writes=[self.ebscB[j]])

    def w_init(self, ntiles):
        self.wseq = self.order * ntiles
        self.wl = 0
        self.wu = 0
        for _ in range(NSLOT):
            self.w_prefetch()

    def w_prefetch(self):
        if self.wl >= len(self.wseq):
            return
        pid = self.wseq[self.wl]
        slot = self.wl % NSLOT
        plen = self.pieces[pid][2]
        while self.cv_next < min(self.npieces, self.wl + CVLOOK + 1):
            self.convert_piece(self.cv_next)
            self.cv_next += 1
        self.S.dma("sync", [DMA(self.WS[slot][:, 0:plen], self.wsc[pid * 128:(pid + 1) * 128, 0:plen])],
                   self.wdsem[slot], reads=[self.wscPB[pid]], writes=[self.WB[slot]])
        self.wl += 1

    def w_get(self, kind=None):
        pid = self.wseq[self.wu]
        if kind is not None:
            assert self.pieces[pid][0] == kind, (self.pieces[pid][0], kind)
        slot = self.wu % NSLOT
        return self.WS[slot], self.WB[slot]

    def w_done(self):
        self.wu += 1
        self.w_prefetch()

    def mmgroup(self, b, mms, reads, chunk_reads=None):
        n = len(mms)
        fns = [MM(o, l, r, i == 0, i == n - 1) for i, (o, l, r) in enumerate(mms)]
        if chunk_reads is not None and self.fresh_norm:
            self.fresh_norm = False
            common = [x for x in reads if x not in chunk_reads]
            for i, fn in enumerate(fns):
                self.S.task("tensor", [fn], reads=common + [chunk_reads[i]], writes=[self.PB[b]])
            return
        self.S.task("tensor", fns, reads=reads, writes=[self.PB[b]])

    def norm(self, src, srcB, grow, N, mode, dst=None, dstB=None):
        S = self.S
        XN, XNB = self.XN, self.XNB
        self.stat_tick(drain=True)
        assert self.st_n == DC and not self.st_q, (self.st_n, len(self.st_q))
        self.st_n = 0
        ps7 = self.ps[:, 7 * 512:7 * 512 + N]
        S.task("scalar", [ACT(self.RS[:, 0:N], ps7, AF.Sqrt, bias=self.epsb[:, 0:1])],
               reads=[self.PB[7], self.CB], writes=[self.RSB])
        S.task("vector", [RCP(ps7, self.RS[:, 0:N])], reads=[self.RSB], writes=[self.PB[7]])
        for c in range(DC):
            if mode == "bf":
                o, oB = XN[:, c, 0:N], [XNB[c]]
            elif mode == "silu":
                o, oB = src[:, c, 0:N], [srcB[c]]
            else:
                o, oB = dst[:, c, 0:N], [dstB[c]]
            S.task("vector", [STT(o, src[:, c, 0:N], self.vrow(c, grow), ps7, ALU.mult, ALU.mult)],
                   reads=[srcB[c], self.PB[7], self.VCB], writes=oB)
            if mode == "silu":
                S.task("scalar", [ACT(XN[:, c, 0:N], src[:, c, 0:N], AF.Silu)], reads=[srcB[c]], writes=[XNB[c]])
        self.fresh_norm = mode != "f32"

    def ffn(self, i, k, N):
        S = self.S
        X, XB, XN, XNB = self.X, self.XB, self.XN, self.XNB
        self.norm(X, XB, R_NORM + i * 4 + (0 if k == 0 else 2), N, "bf")
        H = self.UB[:, 0:FC * T].rearrange("p (c t) -> p c t", c=FC)
        HB = self.HB
        for p in range(11):
            ws, wb = self.w_get("cols")
            w3 = ws[:, :].rearrange("p (c x) -> p c x", c=DC)
            for jj in range(2):
                jf = 2 * p + jj
                ba = self.bank()
                self.mmgroup(ba, [(self.pbank(ba, N), w3[:, c, jj * 128:(jj + 1) * 128], XN[:, c, 0:N]) for c in range(DC)],
                             reads=[wb] + XNB, chunk_reads=XNB)
                bb = self.bank()
                self.mmgroup(bb, [(self.pbank(bb, N), w3[:, c, 256 + jj * 128:256 + (jj + 1) * 128], XN[:, c, 0:N])
                                  for c in range(DC)], reads=[wb] + XNB, chunk_reads=XNB)
                si = self.sa_i
                self.sa_i ^= 1
                S.task("scalar", [ACT(self.SA[:, si, 0:N], self.pbank(ba, N), AF.Silu)], reads=[self.PB[ba]],
                       writes=[self.SAB[si]])
                S.task("vector", [TT(H[:, jf, 0:N], self.pbank(bb, N), self.SA[:, si, 0:N], ALU.mult)],
                       reads=[self.PB[bb], self.SAB[si]], writes=[HB[jf]])
            self.w_done()
        for m in range(DC):
            ws, wb = self.w_get("wout")
            w3 = ws[:, 0:FC * 128].rearrange("p (c x) -> p c x", c=FC)
            b = self.bank()
            self.mmgroup(b, [(self.pbank(b, N), w3[:, jf, :], H[:, jf, 0:N]) for jf in range(FC)], reads=[wb] + HB)
            S.task("vector", [STT(X[:, m, 0:N], self.pbank(b, N), 0.5, X[:, m, 0:N], ALU.mult, ALU.add)],
                   reads=[self.PB[b], XB[m]], writes=[XB[m]])
            self.stat_tick()
            self.stat_add(X, XB, m, N)
            self.w_done()

    def convmix(self, i, N, sample, last):
        S = self.S
        j = i // 2
        X, XB, XN, XNB = self.X, self.XB, self.XN, self.XNB
        I, O = self.I, self.O
        SQ = NSEQ_S if sample else 1
        L = LS if sample else N
        PL = 30 + L
        GLB = self.UB[:, 0:DC * SQ * PL].rearrange("p (c s l) -> p c s l", c=DC, s=SQ)
        GB = self.GB
        yoff = 2240
        YC = self.U[:, yoff:yoff + DC * T].rearrange("p (c t) -> p c t", c=DC)
        YB = self.YB
        O32 = self.O32
        self.norm(X, XB, R_NORM + i * 4 + 1, N, "bf")
        if not sample:
            S.task("gpsimd", [CP(GLB[:, :, 0, 0:30], self.PFX[j][:])], reads=[self.PFXB[j]], writes=GB)
        else:
            for s in range(SQ):
                xi = s % 2
                row0 = (j * NSEQ_S + s) * 30
                S.dma("gpsimd", [DMA(self.XIN[0:30, xi, :], I["cconv"][row0:row0 + 30, :])], self.d_xin[xi],
                      writes=[self.XINB[xi]])
                for half in range(2):
                    b = self.bank()
                    S.task("tensor", [TR(self.ps[:, b * 512 + cc * 32:b * 512 + cc * 32 + 30],
                                         self.XIN[0:30, xi, (half * 4 + cc) * 128:(half * 4 + cc + 1) * 128],
                                         self.identf[0:30, 0:30]) for cc in range(4)],
                           reads=[self.XINB[xi], self.CB], writes=[self.PB[b]])
                    pv = self.ps[:, b * 512:b * 512 + 128].rearrange("p (c r) -> p c r", c=4)
                    S.task("vector", [CP(GLB[:, half * 4:half * 4 + 4, s, 0:30], pv[:, :, 0:30]),
                                      CP(O32[:, half * 4:half * 4 + 4, s, 0:14], pv[:, :, 16:30])],
                           reads=[self.PB[b]], writes=GB + [self.O32B])
        for q in range(4):
            ws, wb = self.w_get("cols")
            w3 = ws[:, :].rearrange("p (c x) -> p c x", c=DC)
            for mm in range(2):
                m = 2 * q + mm
                ba = self.bank()
                self.mmgroup(ba, [(self.pbank(ba, N), w3[:, c, mm * 128:(mm + 1) * 128], XN[:, c, 0:N]) for c in range(DC)],
                             reads=[wb] + XNB, chunk_reads=XNB)
                bg = self.bank()
                self.mmgroup(bg, [(self.pbank(bg, N), w3[:, c, 256 + mm * 128:256 + (mm + 1) * 128], XN[:, c, 0:N])
                                  for c in range(DC)], reads=[wb] + XNB, chunk_reads=XNB)
                si = self.sa_i
                self.sa_i ^= 1
                S.task("scalar", [ACT(self.SA[:, si, 0:N], self.pbank(bg, N), AF.Sigmoid,
                                      bias=self.vrow(m, R_CBIN + j * 2 + 1))],
                       reads=[self.PB[bg], self.VCB], writes=[self.SAB[si]])
                pa = self.pbank(ba, N).rearrange("p (s l) -> p s l", s=SQ)
                sa = self.SA[:, si, 0:N].rearrange("p (s l) -> p s l", s=SQ)
                fns = [STT(GLB[:, m, :, 30:30 + L], pa, self.vrow(m, R_CBIN + j * 2), sa, ALU.add, ALU.mult)]
                if sample:
                    fns.append(STT(O32[:, m, :, 14:30], pa, self.vrow(m, R_CBIN + j * 2), sa, ALU.add, ALU.mult))
                elif last:
                    fns.append(STT(O32[:, m, 0:1, 0:30], pa[:, :, L - 30:L], self.vrow(m, R_CBIN + j * 2),
                                   sa[:, :, L - 30:L], ALU.add, ALU.mult))
                S.task("vector", fns, reads=[self.PB[ba], self.SAB[si], self.VCB], writes=[GB[m], self.O32B])
            self.w_done()
        for m in range(DC):
            ws, wb = self.w_get("dw")
            w3 = ws[:, 0:CW * 128].rearrange("p (k x) -> p k x", k=CW)
            b = self.bank()
            po = self.pbank(b, N).rearrange("p (s l) -> p s l", s=SQ)
            self.mmgroup(b, [(po, w3[:, k, :], GLB[:, m, :, k:k + L]) for k in range(CW)], reads=[wb, GB[m]])
            S.task("scalar", [ACT(YC[:, m, 0:N], self.pbank(b, N), AF.Identity, bias=self.vrow(m, R_DWB + j))],
                   reads=[self.PB[b], self.VCB], writes=[YB[m]])
            self.stat_tick()
            self.stat_add(YC, YB, m, N)
            self.w_done()
        self.norm(YC, YB, R_CNG + j, N, "silu")
        for q in range(2):
            ws, wb = self.w_get("cols")
            w3 = ws[:, :].rearrange("p (c x) -> p c x", c=DC)
            for mm in range(4):
                m = 4 * q + mm
                b = self.bank()
                self.mmgroup(b, [(self.pbank(b, N), w3[:, c, mm * 128:(mm + 1) * 128], XN[:, c, 0:N]) for c in range(DC)],
                             reads=[wb] + XNB, chunk_reads=XNB)
                S.task("vector", [STT(X[:, m, 0:N], self.pbank(b, N), self.vrow(m, R_CBOUT + j), X[:, m, 0:N],
                                      ALU.add, ALU.add)], reads=[self.PB[b], XB[m], self.VCB], writes=[XB[m]])
                self.stat_tick()
                self.stat_add(X, XB, m, N)
            self.w_done()
        if not sample and not last:
            S.task("gpsimd", [CP(self.PFX[j][:], GLB[:, :, 0, L:L + 30])], reads=GB, writes=[self.PFXB[j]])
        if sample or last:
            for s in range(SQ):
                yi = self.yo_i
                self.yo_i ^= 1
                for half in range(2):
                    b = self.bank()
                    S.task("tensor", [TR(self.ps[0:30, b * 512 + cc * 128:b * 512 + (cc + 1) * 128],
                                         O32[:, half * 4 + cc, s, :], self.identf[:]) for cc in range(4)],
                           reads=[self.O32B, self.CB], writes=[self.PB[b]])
                    S.task("vector", [CP(self.YOUT[0:30, yi, half * 512:(half + 1) * 512], self.ps[0:30, b * 512:(b + 1) * 512])],
                           reads=[self.PB[b]], writes=[self.YOUTB[yi]])
                if sample:
                    row0 = (j * NSEQ_S + s) * 30
                    dst = O["convs"][row0:row0 + 30, :]
                else:
                    dst = O["convp"][j * 30:(j + 1) * 30, :]
                S.dma("gpsimd", [DMA(dst, self.YOUT[0:30, yi, :])], self.d_yout[yi], reads=[self.YOUTB[yi]],
                      writes=[self.outB])

    def attn_a(self, m, q0, nq, kblocks, oq0):
        S = self.S
        QT = self.QT
        sr = self.sr_i
        self.sr_i ^= 1
        base = sr * 3 * 512
        SR = self.ps[:, base:base + 1280].rearrange("p (h r q) -> p h r q", h=2, r=5)
        SRB = [self.PB[sr * 3 + t] for t in range(3)]
        E = self.EE[sr]
        PT = self.PP[sr]
        EB_ = self.EB
        fns = []
        for (kc0, vs, nk, r) in kblocks:
            for hh in range(2):
                fns.append(MM(SR[0:nk, hh, r, 0:nq], self.KT[hh * 64:(hh + 1) * 64, m, kc0:kc0 + nk],
                              QT[hh * 64:(hh + 1) * 64, m, q0:q0 + nq], True, True))
        S.task("tensor", fns, reads=[self.QTB[m], self.KTprevB, self.KTcurB[m]], writes=SRB)
        full = [kb for kb in kblocks if kb[2] == 128]
        part = [kb for kb in kblocks if kb[2] != 128]
        afn, mfn = [], []
        if full:
            r0, r1 = full[0][3], full[-1][3] + 1
            afn.append(ACT(E[:, :, r0:r1, 0:nq], SR[:, :, r0:r1, 0:nq], AF.Exp))
            mfn.append(TT(PT[:, :, r0:r1, 0:nq], E[:, :, r0:r1, 0:nq], EB_[:, 2 * m:2 * m + 2, r0:r1, 0:nq], ALU.mult))
        for (kc0, vs, nk, r) in part:
            afn.append(ACT(E[0:nk, :, r, 0:nq], SR[0:nk, :, r, 0:nq], AF.Exp))
            mfn.append(TT(PT[0:nk, :, r, 0:nq], E[0:nk, :, r, 0:nq], EB_[0:nk, 2 * m:2 * m + 2, r, 0:nq], ALU.mult))
        S.task("scalar", afn, reads=SRB, writes=[self.EEB[sr]])
        S.task("vector", mfn, reads=[self.EEB[sr], self.EBB], writes=[self.PPB[sr]])
        return (m, nq, kblocks, oq0, sr)

    def attn_b(self, st):
        S = self.S
        m, nq, kblocks, oq0, sr = st
        OT = self.OT
        PT = self.PP[sr]
        ob = 6 + self.ob_i
        self.ob_i ^= 1
        OBk = self.ps[:, ob * 512:ob * 512 + 256]
        fns = []
        nkb = len(kblocks)
        for hh in range(2):
            for idx, (kc0, vs, nk, r) in enumerate(kblocks):
                fns.append(MM(OBk[hh * 64:(hh + 1) * 64, 0:nq], self.V[0:nk, vs, (2 * m + hh) * 64:(2 * m + hh + 1) * 64],
                              PT[0:nk, hh, r, 0:nq], idx == 0, idx == nkb - 1))
        for hh in range(2):
            for idx, (kc0, vs, nk, r) in enumerate(kblocks):
                fns.append(MM(OBk[hh * 64:(hh + 1) * 64, 128:128 + nq], self.ones64[0:nk, :],
                              PT[0:nk, hh, r, 0:nq], idx == 0, idx == nkb - 1))
        S.task("tensor", fns, reads=[self.PPB[sr], self.VprevB, self.CB] + self.VcurB, writes=[self.PB[ob]])
        ri = self.rd_i
        self.rd_i ^= 1
        S.task("vector", [RCP(self.RD[:, ri, 0:nq], OBk[:, 128:128 + nq])], reads=[self.PB[ob]], writes=[self.RDB[ri]])
        S.task("vector", [TT(OT[:, m, oq0:oq0 + nq], OBk[:, 0:nq], self.RD[:, ri, 0:nq], ALU.mult)],
               reads=[self.PB[ob], self.RDB[ri]], writes=[self.OTB[m]])

    def attn_run(self, blocks):
        prev = None
        for args in blocks:
            st = self.attn_a(*args)
            if prev is not None:
                self.attn_b(prev)
            prev = st
        if prev is not None:
            self.attn_b(prev)

    def attnmix(self, i, N, sample, tile, last):
        S = self.S
        j = i // 2
        I, O = self.I, self.O
        X, XB, XN, XNB = self.X, self.XB, self.XN, self.XNB
        KT, V = self.KT, self.V
        self.QT = self.UB[:, 0:DC * T].rearrange("p (c t) -> p c t", c=DC)
        self.OT = self.UB[:, DC * T:2 * DC * T].rearrange("p (c t) -> p c t", c=DC)
        eoff = 2 * DC * T
        self.EE = [self.UB[:, eoff + s * 1280:eoff + (s + 1) * 1280].rearrange("p (h r q) -> p h r q", h=2, r=5)
                   for s in range(2)]
        self.PP = [self.UB[:, eoff + 2560 + s * 1280:eoff + 2560 + (s + 1) * 1280].rearrange("p (h r q) -> p h r q", h=2, r=5)
                   for s in range(2)]
        self.norm(X, XB, R_NORM + i * 4 + 1, N, "bf")
        for q in range(2):
            ws, wb = self.w_get("cols")
            w3 = ws[:, :].rearrange("p (c x) -> p c x", c=DC)
            for mm in range(4):
                m = 4 * q + mm
                b = self.bank()
                self.mmgroup(b, [(self.pbank(b, N), w3[:, c, mm * 128:(mm + 1) * 128], XN[:, c, 0:N]) for c in range(DC)],
                             reads=[wb] + XNB, chunk_reads=XNB)
                S.task("scalar", [ACT(self.QT[:, m, 0:N], self.pbank(b, N), AF.Copy, scale=HD ** -0.5)],
                       reads=[self.PB[b]], writes=[self.QTB[m]])
            self.w_done()
        if self.stop_at(2):
            return self.skip_w(6)
        ntb = 1 if sample else N // 128
        tbw = NS if sample else 128
        for q in range(2):
            ws, wb = self.w_get("cols")
            w3 = ws[:, :].rearrange("p (c x) -> p c x", c=DC)
            for mm in range(4):
                m = 4 * q + mm
                b = self.bank()
                self.mmgroup(b, [(self.pbank(b, N), w3[:, c, mm * 128:(mm + 1) * 128], XN[:, c, 0:N]) for c in range(DC)],
                             reads=[wb] + XNB, chunk_reads=XNB)
                S.task("scalar", [ACT(KT[:, m, T:T + N], self.pbank(b, N), AF.Copy)],
                       reads=[self.PB[b]], writes=[self.KTcurB[m]])
            if sample or last:
                for tb in range(ntb):
                    b = self.bank()
                    self.mmgroup(b, [(self.ps[0:tbw, b * 512:(b + 1) * 512], XN[:, c, tb * 128:tb * 128 + tbw], w3[:, c, :])
                                     for c in range(DC)], reads=[wb] + XNB, chunk_reads=XNB)
                    yi = self.yo_i
                    self.yo_i ^= 1
                    S.task("vector", [CP(self.YOUT[0:tbw, yi, 0:512], self.ps[0:tbw, b * 512:(b + 1) * 512])],
                           reads=[self.PB[b]], writes=[self.YOUTB[yi]])
                    if sample:
                        dst = O["ks"][j * NS:(j + 1) * NS, q * 512:(q + 1) * 512]
                    else:
                        dst = O["kp"][j * KVW + tb * 128:j * KVW + (tb + 1) * 128, q * 512:(q + 1) * 512]
                    S.dma("gpsimd", [DMA(dst, self.YOUT[0:tbw, yi, 0:512])], self.d_yout[yi], reads=[self.YOUTB[yi]],
                          writes=[self.outB])
            self.w_done()
        if self.stop_at(3):
            return self.skip_w(4)
        for q in range(2):
            ws, wb = self.w_get("cols")
            w3 = ws[:, :].rearrange("p (c x) -> p c x", c=DC)
            nvb = NSEQ_S if sample else N // 128
            vw = LS if sample else 128
            for tb in range(nvb):
                b = self.bank()
                self.mmgroup(b, [(self.ps[0:vw, b * 512:(b + 1) * 512], XN[:, c, tb * vw:(tb + 1) * vw], w3[:, c, :])
                                 for c in range(DC)], reads=[wb] + XNB, chunk_reads=XNB)
                S.task("scalar", [ACT(V[0:vw, 4 + tb, q * 512:(q + 1) * 512], self.ps[0:vw, b * 512:(b + 1) * 512], AF.Copy)],
                       reads=[self.PB[b]], writes=[self.VcurB[tb]])
                if sample or last:
                    yi = self.yo_i
                    self.yo_i ^= 1
                    S.task("vector", [CP(self.YOUT[0:vw, yi, 0:512], self.ps[0:vw, b * 512:(b + 1) * 512])],
                           reads=[self.PB[b]], writes=[self.YOUTB[yi]])
                    if sample:
                        dst = O["vs"][j * NS + tb * LS:j * NS + (tb + 1) * LS, q * 512:(q + 1) * 512]
                    else:
                        dst = O["vp"][j * KVW + tb * 128:j * KVW + (tb + 1) * 128, q * 512:(q + 1) * 512]
                    S.dma("gpsimd", [DMA(dst, self.YOUT[0:vw, yi, 0:512])], self.d_yout[yi], reads=[self.YOUTB[yi]],
                          writes=[self.outB])
            self.w_done()
        if self.stop_at(4):
            return self.skip_w(2)
        if not sample:
            blocks = []
            for qb in range(N // 128):
                kbs = []
                for r in range(5):
                    t = qb + r - 4
                    if t < 0 and tile == 0:
                        continue
                    kt = 4 + t
                    kc0 = (kt * 128) if kt < 4 else (T + (kt - 4) * 128)
                    kbs.append((kc0, kt, 128, r))
                blocks += [(m, qb * 128, 128, kbs, qb * 128) for m in range(DC)]
            self.attn_run(blocks)
        else:
            for s in range(NSEQ_S):
                self.load_sample_kv(j, s)
                kbs = [(r * 128, r, 128, r) for r in range(4)] + [(T + s * LS, 4 + s, LS, 4)]
                self.attn_run([(m, s * LS, LS, kbs, s * LS) for m in range(DC)])
        if self.stop_at(5):
            return self.skip_w(2)
        for q in range(2):
            ws, wb = self.w_get("cols")
            w3 = ws[:, :].rearrange("p (c x) -> p c x", c=DC)
            for mm in range(4):
                m = 4 * q + mm
                b = self.bank()
                self.mmgroup(b, [(self.pbank(b, N), w3[:, c, mm * 128:(mm + 1) * 128], self.OT[:, c, 0:N]) for c in range(DC)],
                             reads=[wb] + self.OTB)
                S.task("vector", [TT(X[:, m, 0:N], self.pbank(b, N), X[:, m, 0:N], ALU.add)],
                       reads=[self.PB[b], XB[m]], writes=[XB[m]])
                self.stat_tick()
                self.stat_add(X, XB, m, N)
            self.w_done()
        if self.stop_at(6):
            return
        if not sample and not last:
            S.dma("gpsimd", [DMA(self.kvsc[(2 * j) * 128:(2 * j + 1) * 128, :].rearrange("p (c t) -> p c t", c=DC),
                                 KT[:, :, T:2 * T]),
                             DMA(self.kvsc[(2 * j + 1) * 128:(2 * j + 2) * 128, :].rearrange("p (k d) -> p k d", k=4),
                                 V[:, 4:8, :])],
                  self.d_kvs, reads=self.KTcurB + self.VcurB, writes=[self.kvscB[2 * j], self.kvscB[2 * j + 1]])

    def skip_w(self, n):
        for _ in range(n):
            self.w_get()
            self.w_done()

    def load_prev_kv(self, j):
        S = self.S
        S.dma("gpsimd", [DMA(self.KT[:, :, 0:T], self.kvsc[(2 * j) * 128:(2 * j + 1) * 128, :].rearrange("p (c t) -> p c t", c=DC)),
                         DMA(self.V[:, 0:4, :], self.kvsc[(2 * j + 1) * 128:(2 * j + 2) * 128, :].rearrange("p (k d) -> p k d", k=4))],
              self.d_kvl, reads=[self.kvscB[2 * j], self.kvscB[2 * j + 1]], writes=[self.KTprevB, self.VprevB])

    def load_eb(self, j):
        self.S.dma("gpsimd", [DMA(self.EB[:].rearrange("p h r q -> p (h r q)"), self.ebsc[j * 128:(j + 1) * 128, :])],
                   self.d_eb, reads=[self.ebscB[j]], writes=[self.EBB])

    def load_sample_kv(self, j, s):
        S = self.S
        I = self.I
        row0 = (j * NSEQ_S + s) * KVW
        S.dma("gpsimd", [DMA(self.V[:, 0:4, :], I["cv"][row0:row0 + KVW, :].rearrange("(k p) d -> p k d", p=128))],
              self.d_kvl, writes=[self.VprevB])
        S.dma("gpsimd", [DMA(self.KST[:], I["ck"][row0:row0 + KVW, :].rearrange("(k p) d -> p k d", p=128))],
              self.d_kst, writes=[self.KSTB])
        for c in range(DC):
            b = self.bank()
            S.task("tensor", [TR(self.psb[:, b * 1024 + kt * 128:b * 1024 + (kt + 1) * 128],
                                 self.KST[:, kt, c * 128:(c + 1) * 128], self.identb[:]) for kt in range(4)],
                   reads=[self.KSTB, self.CB], writes=[self.PB[b]])
            S.task("vector", [CP(self.KT[:, c, 0:T], self.psb[:, b * 1024:b * 1024 + 512])],
                   reads=[self.PB[b]], writes=[self.KTprevB])

    def load_p(self, i, sample, tile):
        I = self.I
        if sample:
            self.S.dma("gpsimd", [DMA(self.PIN[0:NS, 0, :], I["psm"][i * NS:(i + 1) * NS, :])], self.d_pin,
                       writes=[self.PINB])
        else:
            r0 = i * self.SEQ + tile * T
            self.S.dma("gpsimd", [DMA(self.PIN[:], I["pp"][r0:r0 + T, :].rearrange("(b p) d -> p b d", p=128))],
                       self.d_pin, writes=[self.PINB])

    def pegate(self, i, N, sample):
        S = self.S
        X, XB, XN, XNB = self.X, self.XB, self.XN, self.XNB
        nb = 1 if sample else N // 128
        bw = NS if sample else 128
        for half in range(2):
            b = self.bank()
            S.task("tensor", [TR(self.ps[:, b * 512 + tb * 128:b * 512 + tb * 128 + bw],
                                 self.PIN[0:bw, tb, half * 128:(half + 1) * 128], self.identf[0:bw, 0:bw])
                              for tb in range(nb)], reads=[self.PINB, self.CB], writes=[self.PB[b]])
            S.task("scalar", [ACT(self.PTT[:, half, 0:N], self.pbank(b, N), AF.Copy)], reads=[self.PB[b]],
                   writes=[self.PTTB])
        PPJ = self.U[:, 0:DC * T].rearrange("p (c t) -> p c t", c=DC)
        TMP = self.U[:, DC * T:DC * T + 2 * T].rearrange("p (s t) -> p s t", s=2)
        ws, wb = self.w_get("proj")
        w3 = ws[:, 0:2 * D].rearrange("p (c x) -> p c x", c=2)
        for m in range(DC):
            b = self.bank()
            self.mmgroup(b, [(self.pbank(b, N), w3[:, cc, m * 128:(m + 1) * 128], self.PTT[:, cc, 0:N]) for cc in range(2)],
                         reads=[wb, self.PTTB])
            S.task("scalar", [ACT(PPJ[:, m, 0:N], self.pbank(b, N), AF.Copy)], reads=[self.PB[b]], writes=[self.PPJB[m]])
        self.w_done()
        self.norm(X, XB, R_NORM + i * 4 + 3, N, "bf")
        for q in range(2):
            ws, wb = self.w_get("cols")
            w3 = ws[:, :].rearrange("p (c x) -> p c x", c=DC)
            for mm in range(4):
                m = 4 * q + mm
                b = self.bank()
                self.mmgroup(b, [(self.pbank(b, N), w3[:, c, mm * 128:(mm + 1) * 128], XN[:, c, 0:N]) for c in range(DC)],
                             reads=[wb] + XNB, chunk_reads=XNB)
                si = self.sa_i
                self.sa_i ^= 1
                S.task("scalar", [ACT(self.SA[:, si, 0:N], self.pbank(b, N), AF.Sigmoid)], reads=[self.PB[b]],
                       writes=[self.SAB[si]])
                S.task("vector", [TT(TMP[:, si, 0:N], self.SA[:, si, 0:N], PPJ[:, m, 0:N], ALU.mult)],
                       reads=[self.SAB[si], self.PPJB[m]], writes=[self.TMPB[si]])
                S.task("gpsimd", [TT(X[:, m, 0:N], X[:, m, 0:N], TMP[:, si, 0:N], ALU.add)],
                       reads=[self.TMPB[si], XB[m]], writes=[XB[m]])
                self.stat_tick()
                self.stat_add(X, XB, m, N)
            self.w_done()

    def load_x(self, sample, tile):
        S = self.S
        I = self.I
        nb = 1 if sample else 4
        bw = NS if sample else 128
        for tb in range(nb):
            xi = tb % 2
            src = I["xs"] if sample else I["xp"][tile * T + tb * 128:tile * T + (tb + 1) * 128, :]
            S.dma("gpsimd", [DMA(self.XIN[0:bw, xi, :], src)], self.d_xin[xi], writes=[self.XINB[xi]])
            for half in range(2):
                b = self.bank()
                S.task("tensor", [TR(self.ps[:, b * 512 + cc * 128:b * 512 + cc * 128 + bw],
                                     self.XIN[0:bw, xi, (half * 4 + cc) * 128:(half * 4 + cc + 1) * 128],
                                     self.identf[0:bw, 0:bw]) for cc in range(4)],
                       reads=[self.XINB[xi], self.CB], writes=[self.PB[b]])
                pv = self.ps[:, b * 512:(b + 1) * 512].rearrange("p (c t) -> p c t", c=4)
                S.task("vector", [CP(self.X[:, half * 4:half * 4 + 4, tb * 128:tb * 128 + bw], pv[:, :, 0:bw])],
                       reads=[self.PB[b]], writes=self.XB[half * 4:half * 4 + 4])
        for c in range(DC):
            self.stat_add(self.X, self.XB, c, NS if sample else T)

    def store_y(self, N, sample, tile):
        S = self.S
        O = self.O
        YF = self.U[:, 0:DC * T].rearrange("p (c t) -> p c t", c=DC)
        self.norm(self.X, self.XB, R_FINAL, N, "f32", dst=YF, dstB=self.YFB)
        nb = 1 if sample else 4
        bw = NS if sample else 128
        for tb in range(nb):
            yi = self.yo_i
            self.yo_i ^= 1
            for half in range(2):
                b = self.bank()
                S.task("tensor", [TR(self.ps[0:bw, b * 512 + cc * 128:b * 512 + (cc + 1) * 128],
                                     YF[:, half * 4 + cc, tb * 128:tb * 128 + bw], self.identf[:]) for cc in range(4)],
                       reads=self.YFB + [self.CB], writes=[self.PB[b]])
                S.task("scalar", [ACT(self.YOUT[0:bw, yi, half * 512:(half + 1) * 512], self.ps[0:bw, b * 512:(b + 1) * 512],
                                      AF.Copy)], reads=[self.PB[b]], writes=[self.YOUTB[yi]])
            dst = O["ys"] if sample else O["yp"][tile * T + tb * 128:tile * T + (tb + 1) * 128, :]
            S.dma("gpsimd", [DMA(dst, self.YOUT[0:bw, yi, :])], self.d_yout[yi], reads=[self.YOUTB[yi]], writes=[self.outB])

    def _main(self):
        S = self.S
        self.epsb = self.sb("epsb", [128, 1], F32)
        S.task("vector", [MS(self.epsb[:], EPS)], writes=[self.CB])
        self.HB = [Buf("H%d" % f) for f in range(FC)]
        self.GB = [Buf("G%d" % c) for c in range(DC)]
        self.YB = [Buf("YC%d" % c) for c in range(DC)]
        self.QTB = [Buf("QT%d" % c) for c in range(DC)]
        self.OTB = [Buf("OT%d" % c) for c in range(DC)]
        self.EEB = [Buf("EE0"), Buf("EE1")]
        self.PPB = [Buf("PP0"), Buf("PP1")]
        self.PPJB = [Buf("PPJ%d" % c) for c in range(DC)]
        self.TMPB = [Buf("TMP0"), Buf("TMP1")]
        self.YFB = [Buf("YF%d" % c) for c in range(DC)]
        self.outB = Buf("out")
        self.sr_i = 0
        self.ob_i = 0
        tiles = [(True, 0)] + [(False, t) for t in range(self.NT)]
        self.w_init(len(tiles))
        import os
        dbg = os.environ.get("KDBG")
        dbg = tuple(int(v) for v in dbg.split(",")) if dbg else None
        self.dbgp = None
        for ti, (sample, tile) in enumerate(tiles):
            if dbg is not None and (ti > dbg[0] or (ti == dbg[0] and dbg[1] == 0)):
                break
            N = NS if sample else T
            last = (not sample) and tile == self.NT - 1
            self.load_x(sample, tile)
            for i in range(DEPTH):
                j = i // 2
                if dbg is not None and ti == dbg[0] and i >= dbg[1]:
                    break
                self.dbgp = (dbg[2],) if (dbg is not None and len(dbg) > 2 and ti == dbg[0] and i == dbg[1] - 1) else None
                self.load_p(i, sample, tile)
                if i % 2 == 1:
                    self.load_eb(j)
                    if not sample and tile > 0:
                        self.load_prev_kv(j)
                self.uphase()
                self.ffn(i, 0, N)
                self.uphase()
                if i % 2 == 0:
                    self.convmix(i, N, sample, last)
                else:
                    self.attnmix(i, N, sample, tile, last)
                self.uphase()
                self.ffn(i, 1, N)
                self.uphase()
                self.pegate(i, N, sample)
            if dbg is not None and ti == dbg[0]:
                break
            self.uphase()
            self.store_y(N, sample, tile)
        assert dbg is not None or self.wu == len(self.wseq), (self.wu, len(self.wseq))

    def stop_at(self, k):
        d = self.dbgp
        return d is not None and d[0] == k

    def uphase(self):
        allb = (self.HB + self.GB + self.YB + self.QTB + self.OTB + self.EEB + self.PPB + self.PPJB + self.TMPB + self.YFB)
        toks = {}
        for b in allb:
            for t in ([b.w] if b.w else []) + b.r:
                if toks.get(t[0], 0) < t[1]:
                    toks[t[0]] = t[1]
        for b in allb:
            b.w = None
            b.r = list(toks.items())


_NC_CACHE = {}


def _get_nc(SEQ):
    if SEQ not in _NC_CACHE:
        _NC_CACHE[SEQ] = Builder(SEQ).build()
    return _NC_CACHE[SEQ]


def kernel(x_prompt, x_sample, cache_conv, cache_k, cache_v, p_prompt, p_sample,
           norm_g, final_g, ffn_w_in, ffn_w_out, conv_w_in, conv_b_in, conv_dw, conv_dw_b,
           conv_norm_g, conv_w_out, conv_b_out, attn_w_qkv, attn_w_o, attn_rel_table,
           pe_w_proj, pe_w_gate):
    f = lambda a: np.ascontiguousarray(np.asarray(a, dtype=np.float32))
    B, SEQ, _ = x_prompt.shape
    NCORE = 8
    assert B == NCORE and x_sample.shape[0] == NCORE * NSEQ_S and x_sample.shape[1] == LS
    nc = _get_nc(SEQ)
    shared = {
        "norm_g": f(norm_g).reshape(16, D), "final_g": f(final_g).reshape(1, D),
        "ffn_w_in": f(ffn_w_in).reshape(8 * D, 2 * DFF), "ffn_w_out": f(ffn_w_out).reshape(8 * DFF, D),
        "conv_w_in": f(conv_w_in).reshape(2 * D, 2 * D), "conv_b_in": f(conv_b_in).reshape(4, D),
        "conv_dw": f(conv_dw).reshape(2 * CW, D), "conv_dw_b": f(conv_dw_b).reshape(2, D),
        "conv_norm_g": f(conv_norm_g).reshape(2, D), "conv_w_out": f(conv_w_out).reshape(2 * D, D),
        "conv_b_out": f(conv_b_out).reshape(2, D), "attn_w_qkv": f(attn_w_qkv).reshape(2 * D, 3 * D),
        "attn_w_o": f(attn_w_o).reshape(2 * D, D), "rel_table": f(attn_rel_table).reshape(2 * NH, 257),
        "pe_w_proj": f(pe_w_proj).reshape(DEPTH * PED, D), "pe_w_gate": f(pe_w_gate).reshape(DEPTH * D, D),
        "ident": np.eye(128, dtype=np.float32),
    }
    xp, xs = f(x_prompt), f(x_sample)
    cc, ck, cv = f(cache_conv), f(cache_k), f(cache_v)
    pp, psm = f(p_prompt), f(p_sample)
    in_maps = []
    for c in range(NCORE):
        sl = slice(c * NSEQ_S, (c + 1) * NSEQ_S)
        m = dict(shared)
        m["xp"] = xp[c]
        m["xs"] = xs[sl].reshape(NS, D)
        m["cconv"] = cc[:, sl].reshape(2 * NSEQ_S * 30, D)
        m["ck"] = ck[:, sl].reshape(2 * NSEQ_S * KVW, D)
        m["cv"] = cv[:, sl].reshape(2 * NSEQ_S * KVW, D)
        m["pp"] = pp[:, c].reshape(DEPTH * SEQ, PED)
        m["psm"] = psm[:, sl].reshape(DEPTH * NS, PED)
        in_maps.append(m)
    res = run_bass_kernel_spmd(nc, in_maps, core_ids=list(range(NCORE)))
    R = res.results
    y_prompt = np.stack([R[c]["yp"] for c in range(NCORE)], 0)
    y_sample = np.concatenate([R[c]["ys"].reshape(NSEQ_S, LS, D) for c in range(NCORE)], 0)
    conv_prompt = np.stack([R[c]["convp"].reshape(2, 30, D) for c in range(NCORE)], 1)
    k_prompt = np.stack([R[c]["kp"].reshape(2, KVW, NH, HD) for c in range(NCORE)], 1)
    v_prompt = np.stack([R[c]["vp"].reshape(2, KVW, NH, HD) for c in range(NCORE)], 1)
    conv_sample = np.concatenate([R[c]["convs"].reshape(2, NSEQ_S, 30, D) for c in range(NCORE)], 1)
    k_sample = np.concatenate([R[c]["ks"].reshape(2, NSEQ_S, LS, NH, HD) for c in range(NCORE)], 1)
    v_sample = np.concatenate([R[c]["vs"].reshape(2, NSEQ_S, LS, NH, HD) for c in range(NCORE)], 1)
    return (y_prompt, y_sample, conv_prompt, k_prompt, v_prompt, conv_sample, k_sample, v_sample)
```

```python
import numpy as np
from contextlib import ExitStack
import concourse.bass as bass
import concourse.mybir as mybir
from concourse.bass_utils import run_bass_kernel_spmd

F32 = mybir.dt.float32
BF16 = mybir.dt.bfloat16
AF = mybir.ActivationFunctionType
ALU = mybir.AluOpType

ENGS = ["tensor", "vector", "scalar", "gpsimd", "sync"]

D = 1024
DC = 8
DFF = 2816
FC = 22
T = 512
NH = 16
HD = 64
DEPTH = 4
CW = 31
PED = 256
NSLOT = 6
NCV = 8
CVLOOK = 16
EPS = 1e-6
NSEQ_S = 4
LS = 16
NS = NSEQ_S * LS
KVW = 512


class Buf:
    __slots__ = ("name", "w", "r", "excl")

    def __init__(self, name, excl=False):
        self.name = name
        self.w = None
        self.r = []
        self.excl = excl


class DSem:
    __slots__ = ("sem", "count", "key")

    def __init__(self, sem, key):
        self.sem = sem
        self.count = 0
        self.key = key


class Sched:
    def __init__(self, nc, eng_sems):
        self.nc = nc
        self.q = {e: [] for e in ENGS}
        self.esem = eng_sems
        self.cnt = {e: 0 for e in ENGS}
        self.seen = {e: {} for e in ENGS}
        self.semobj = dict(eng_sems)
        self.dsems = []

    def new_dsem(self, sem, name):
        d = DSem(sem, "d:" + name)
        self.semobj[d.key] = sem
        self.dsems.append(d)
        return d

    def _waits_for(self, eng, reads, writes):
        need = {}

        def add(tok):
            if tok is None:
                return
            k, v = tok
            if need.get(k, 0) < v:
                need[k] = v

        for b in reads:
            add(b.w)
            if b.excl:
                for t in b.r:
                    if t[0] != eng:
                        add(t)
        for b in writes:
            add(b.w)
            for t in b.r:
                add(t)
        out = []
        seen = self.seen[eng]
        for k, v in need.items():
            if k == "tensor" and eng == "tensor":
                continue
            if seen.get(k, 0) >= v:
                continue
            seen[k] = v
            out.append((k, v))
        return out

    def _record(self, tok, reads, writes):
        for b in reads:
            if len(b.r) > 24:
                best = {}
                for k, v in b.r:
                    if best.get(k, 0) < v:
                        best[k] = v
                b.r = list(best.items())
            b.r.append(tok)
        for b in writes:
            b.w = tok
            b.r = []

    def task(self, eng, fns, reads=(), writes=()):
        waits = self._waits_for(eng, reads, writes)
        self.cnt[eng] += 1
        tok = (eng, self.cnt[eng])
        self._record(tok, reads, writes)
        self.q[eng].append((waits, fns, None))
        return tok

    def dma(self, eng, fns, dsem, reads=(), writes=()):
        waits = self._waits_for(eng, reads, writes)
        dsem.count += 16 * len(fns)
        tok = (dsem.key, dsem.count)
        self._record(tok, reads, writes)
        self.q[eng].append((waits, fns, dsem))
        return tok

    def _all_tokens(self):
        toks = [(e, self.cnt[e]) for e in ENGS if self.cnt[e] > 0]
        toks += [(d.key, d.count) for d in self.dsems if d.count > 0]
        return toks

    def barrier(self):
        toks = self._all_tokens()
        for e in ENGS:
            waits = []
            for k, v in toks:
                if k == e and e == "tensor":
                    continue
                if self.seen[e].get(k, 0) >= v:
                    continue
                self.seen[e][k] = v
                waits.append((k, v))
            if waits:
                self.q[e].append((waits, [], None))

    def final_wait(self, eng="sync"):
        self.q[eng].append((list(self._all_tokens()), [], None))

    def emit(self, block):
        def run(engname, e):
            own = self.esem.get(engname)
            for waits, fns, dsem in self.q[engname]:
                for k, v in waits:
                    e.wait_ge(self.semobj[k], v)
                n = len(fns)
                for i, fn in enumerate(fns):
                    ins = fn(e)
                    if dsem is not None:
                        ins.then_inc(dsem.sem, 16)
                    elif i == n - 1:
                        ins.then_inc(own, 1)

        @block.tensor
        def _(e):
            run("tensor", e)

        @block.vector
        def _(e):
            run("vector", e)

        @block.scalar
        def _(e):
            run("scalar", e)

        @block.gpsimd
        def _(e):
            run("gpsimd", e)

        @block.sync
        def _(e):
            run("sync", e)


def MM(out, lhsT, rhs, start, stop):
    return lambda e: e.matmul(out=out, lhsT=lhsT, rhs=rhs, start=start, stop=stop)


def TR(out, in_, ident):
    return lambda e: e.transpose(out=out, in_=in_, identity=ident)


def ACT(out, in_, func, bias=None, scale=None):
    kw = {}
    if bias is not None:
        kw["bias"] = bias
    if scale is not None:
        kw["scale"] = scale
    return lambda e: e.activation(out=out, in_=in_, func=func, **kw)


def TT(out, in0, in1, op):
    return lambda e: e.tensor_tensor(out=out, in0=in0, in1=in1, op=op)


def STT(out, in0, scalar, in1, op0, op1):
    return lambda e: e.scalar_tensor_tensor(out=out, in0=in0, scalar=scalar, in1=in1, op0=op0, op1=op1)


def TS(out, in0, scalar1, op0, scalar2=None, op1=None):
    if op1 is None:
        return lambda e: e.tensor_scalar(out=out, in0=in0, scalar1=scalar1, scalar2=None, op0=op0)
    return lambda e: e.tensor_scalar(out=out, in0=in0, scalar1=scalar1, scalar2=scalar2, op0=op0, op1=op1)


def CP(out, in_):
    return lambda e: e.tensor_copy(out=out, in_=in_)


def RCP(out, in_):
    return lambda e: e.reciprocal(out=out, in_=in_)


def MS(ap, v):
    return lambda e: e.memset(ap, v)


def DMA(out, in_):
    return lambda e: e.dma_start(out=out, in_=in_)


R_NORM = 0
R_FINAL = 16
R_CBIN = 17
R_DWB = 21
R_CNG = 23
R_CBOUT = 25
R_DW = 27
NVROW = 27 + 2 * CW


class Builder:
    def __init__(self, SEQ):
        assert SEQ % T == 0 and SEQ >= T
        self.SEQ = SEQ
        self.NT = SEQ // T
        self.nc = bass.Bass("TRN2", target_bir_lowering=False)
        self.es = ExitStack()

    def din(self, name, shape, dt=F32):
        return self.nc.dram_tensor(name, list(shape), dt, kind="ExternalInput").ap()

    def dout(self, name, shape, dt=F32):
        return self.nc.dram_tensor(name, list(shape), dt, kind="ExternalOutput").ap()

    def dscr(self, name, shape, dt):
        return self.nc.dram_tensor(name, list(shape), dt, kind="Internal").ap()

    def sb(self, name, shape, dt):
        return self.es.enter_context(self.nc.sbuf_tensor(name, list(shape), dt))

    def sem(self, name):
        return self.es.enter_context(self.nc.semaphore(name))

    def dsem(self, name):
        return self.S.new_dsem(self.sem("d_" + name), name)

    def build(self):
        nc = self.nc
        SEQ, NT = self.SEQ, self.NT
        with self.es:
            self._declare_dram()
            self._alloc()
            self._pieces()
            self._prologue()
            self.S.barrier()
            self._main()
            self.S.final_wait("sync")
            with nc.Block() as block:
                self.S.emit(block)
        return nc

    def _declare_dram(self):
        SEQ = self.SEQ
        I = self.I = {}
        I["xp"] = self.din("xp", [SEQ, D])
        I["xs"] = self.din("xs", [NS, D])
        I["cconv"] = self.din("cconv", [2 * NSEQ_S * 30, D])
        I["ck"] = self.din("ck", [2 * NSEQ_S * KVW, D])
        I["cv"] = self.din("cv", [2 * NSEQ_S * KVW, D])
        I["pp"] = self.din("pp", [DEPTH * SEQ, PED])
        I["psm"] = self.din("psm", [DEPTH * NS, PED])
        I["norm_g"] = self.din("norm_g", [16, D])
        I["final_g"] = self.din("final_g", [1, D])
        I["ffn_w_in"] = self.din("ffn_w_in", [8 * D, 2 * DFF])
        I["ffn_w_out"] = self.din("ffn_w_out", [8 * DFF, D])
        I["conv_w_in"] = self.din("conv_w_in", [2 * D, 2 * D])
        I["conv_b_in"] = self.din("conv_b_in", [4, D])
        I["conv_dw"] = self.din("conv_dw", [2 * CW, D])
        I["conv_dw_b"] = self.din("conv_dw_b", [2, D])
        I["conv_norm_g"] = self.din("conv_norm_g", [2, D])
        I["conv_w_out"] = self.din("conv_w_out", [2 * D, D])
        I["conv_b_out"] = self.din("conv_b_out", [2, D])
        I["attn_w_qkv"] = self.din("attn_w_qkv", [2 * D, 3 * D])
        I["attn_w_o"] = self.din("attn_w_o", [2 * D, D])
        I["rel_table"] = self.din("rel_table", [2 * NH, 257])
        I["pe_w_proj"] = self.din("pe_w_proj", [DEPTH * PED, D])
        I["pe_w_gate"] = self.din("pe_w_gate", [DEPTH * D, D])
        I["ident"] = self.din("ident", [128, 128])
        O = self.O = {}
        O["yp"] = self.dout("yp", [SEQ, D])
        O["ys"] = self.dout("ys", [NS, D])
        O["convp"] = self.dout("convp", [2 * 30, D])
        O["kp"] = self.dout("kp", [2 * KVW, D])
        O["vp"] = self.dout("vp", [2 * KVW, D])
        O["convs"] = self.dout("convs", [2 * NSEQ_S * 30, D])
        O["ks"] = self.dout("ks", [2 * NS, D])
        O["vs"] = self.dout("vs", [2 * NS, D])

    def _alloc(self):
        nc = self.nc
        sb = self.sb
        esems = {e: self.sem("s_" + e) for e in ENGS}
        self.S = Sched(nc, esems)
        self.X = sb("X", [128, DC, T], F32)
        self.XB = [Buf("X%d" % c) for c in range(DC)]
        self.XN = sb("XN", [128, DC, T], BF16)
        self.XNB = [Buf("XN%d" % c) for c in range(DC)]
        self.RS = sb("RS", [128, T], F32)
        self.RSB = Buf("RS")
        self.SQ = sb("SQ", [128, DC, T], BF16)
        self.SQB = [Buf("SQ%d" % c) for c in range(DC)]
        self.st_q = []
        self.st_n = 0
        self.fresh_norm = False
        self.SA = sb("SA", [128, 2, T], F32)
        self.SAB = [Buf("SA0"), Buf("SA1")]
        self.sa_i = 0
        self.U = sb("U", [128, 6656], F32)
        self.UB = self.U.bitcast(BF16)
        self.KT = sb("KT", [128, DC, 2 * T], BF16)
        self.KTprevB = Buf("KTprev")
        self.KTcurB = [Buf("KTcur%d" % c) for c in range(DC)]
        self.V = sb("V", [128, 8, D], BF16)
        self.VprevB = Buf("Vprev")
        self.VcurB = [Buf("Vcur%d" % t) for t in range(4)]
        self.EB = sb("EB", [128, NH, 5, 128], BF16)
        self.EBB = Buf("EB")
        self.KST = sb("KST", [128, 4, D], BF16)
        self.KSTB = Buf("KST")
        self.WS = [sb("WS%d" % s, [128, 4096], BF16) for s in range(NSLOT)]
        self.WB = [Buf("WS%d" % s) for s in range(NSLOT)]
        self.XIN = sb("XIN", [128, 2, D], F32)
        self.XINB = [Buf("XIN0"), Buf("XIN1")]
        self.YOUT = sb("YOUT", [128, 2, D], F32)
        self.YOUTB = [Buf("YOUT0"), Buf("YOUT1")]
        self.yo_i = 0
        self.PIN = sb("PIN", [128, 4, PED], F32)
        self.PINB = Buf("PIN")
        self.PTT = sb("PTT", [128, 2, T], BF16)
        self.PTTB = Buf("PTT")
        self.identf = sb("identf", [128, 128], F32)
        self.identb = sb("identb", [128, 128], BF16)
        self.onesm = sb("onesm", [128, 128], BF16)
        self.ones64 = sb("ones64", [128, 64], BF16)
        self.CB = Buf("consts")
        self.vecs = sb("vecs", [128, DC, 96], F32)
        self.VCB = Buf("vecs")
        self.PFX = [sb("PFX%d" % j, [128, DC, 30], BF16) for j in range(2)]
        self.PFXB = [Buf("PFX0"), Buf("PFX1")]
        self.O32 = sb("O32", [128, DC, NSEQ_S, 30], F32)
        self.O32B = Buf("O32")
        self.RD = sb("RD", [128, 2, 128], F32)
        self.RDB = [Buf("RD0"), Buf("RD1")]
        self.rd_i = 0
        self.ps = self.es.enter_context(nc.psum_tensor("ps", [128, 4096], F32))
        self.psb = self.ps.bitcast(BF16)
        self.PB = [Buf("PS%d" % b, excl=True) for b in range(8)]
        self.bk = 7
        self.wdsem = [self.dsem("w%d" % s) for s in range(NSLOT)]
        self.d_xin = [self.dsem("xin0"), self.dsem("xin1")]
        self.d_yout = [self.dsem("yout0"), self.dsem("yout1")]
        self.d_pin = self.dsem("pin")
        self.d_misc = self.dsem("misc")
        self.miscB = Buf("miscchain")
        self.d_eb = self.dsem("eb")
        self.d_kvl = self.dsem("kvl")
        self.d_kvs = self.dsem("kvs")
        self.d_kst = self.dsem("kst")
        self.d_o32 = self.dsem("o32")
        self.rtab = self.dscr("rtab", [2 * NH, 385], F32)
        self.rtabB = Buf("rtab")
        self.ebsc = self.dscr("ebsc", [2 * 128, NH * 5 * 128], BF16)
        self.ebscB = [Buf("ebsc0"), Buf("ebsc1")]
        self.kvsc = self.dscr("kvsc", [4 * 128, 4096], BF16)
        self.kvscB = [Buf("kvsc%d" % i) for i in range(4)]

    def bank(self):
        self.bk = (self.bk + 1) % 7
        return self.bk

    def stat_add(self, src, srcB, c, N):
        self.st_q.append([src, srcB, c, N, 0])

    def _stat_sq(self, it):
        src, srcB, c, N, _ = it
        self.S.task("scalar", [ACT(self.SQ[:, c, 0:N], src[:, c, 0:N], AF.Square)], reads=[srcB[c]], writes=[self.SQB[c]])
        it[4] = 1

    def _stat_mm(self, it):
        src, srcB, c, N, _ = it
        assert c == self.st_n
        self.S.task("tensor", [MM(self.ps[:, 7 * 512:7 * 512 + N], self.onesm[:], self.SQ[:, c, 0:N], c == 0, c == DC - 1)],
                    reads=[self.SQB[c], self.CB], writes=[self.PB[7]])
        self.st_n += 1
        it[4] = 2

    def stat_tick(self, drain=False):
        for it in self.st_q:
            if it[4] == 1:
                self._stat_mm(it)
        for it in self.st_q:
            if it[4] == 0:
                self._stat_sq(it)
        if drain:
            for it in self.st_q:
                if it[4] == 1:
                    self._stat_mm(it)
        self.st_q = [it for it in self.st_q if it[4] != 2]

    def mdma(self, fns, reads=(), writes=()):
        return self.S.dma("sync", fns, self.d_misc, reads=list(reads), writes=list(writes) + [self.miscB])

    def pbank(self, b, n=T):
        return self.ps[:, b * 512:b * 512 + n]

    def vrow(self, c, r):
        return self.vecs[:, c, r:r + 1]

    def _pieces(self):
        I = self.I
        P = self.pieces = []
        order = self.order = []

        def add(kind, args, plen):
            P.append((kind, args, plen))
            order.append(len(P) - 1)

        for i in range(DEPTH):
            j = i // 2

            def ffn(k):
                r0 = (i * 2 + k) * D
                for p in range(11):
                    add("cols", (I["ffn_w_in"], r0, 2 * DFF, 512,
                                 [(p * 256, 256, 0), (DFF + p * 256, 256, 256)]), 4096)
                for m in range(DC):
                    add("wout", (I["ffn_w_out"], (i * 2 + k) * DFF, m), FC * 128)

            ffn(0)
            if i % 2 == 0:
                for q in range(4):
                    add("cols", (I["conv_w_in"], j * D, 2 * D, 512,
                                 [(q * 256, 256, 0), (D + q * 256, 256, 256)]), 4096)
                for m in range(DC):
                    add("dw", (j, m), CW * 128)
                for q in range(2):
                    add("cols", (I["conv_w_out"], j * D, D, 512, [(q * 512, 512, 0)]), 4096)
            else:
                for q in range(6):
                    add("cols", (I["attn_w_qkv"], j * D, 3 * D, 512, [(q * 512, 512, 0)]), 4096)
                for q in range(2):
                    add("cols", (I["attn_w_o"], j * D, D, 512, [(q * 512, 512, 0)]), 4096)
            ffn(1)
            add("proj", (i,), 2 * D)
            for q in range(2):
                add("cols", (I["pe_w_gate"], i * D, D, 512, [(q * 512, 512, 0)]), 4096)
        self.npieces = len(P)
        self.wsc = self.dscr("wsc", [self.npieces * 128, 4096], BF16)
        self.wscPB = [Buf("wsc%d" % p) for p in range(self.npieces)]
        self.d_cv = [self.dsem("cv%d" % k) for k in range(NCV)]
        self.cvchain = [Buf("cvchain%d" % k) for k in range(NCV)]
        self.cv_next = 0

    def convert_piece(self, pid, prologue=False):
        S = self.S
        kind, args, plen = self.pieces[pid]
        dstrows = self.wsc[pid * 128:(pid + 1) * 128, :]
        k = pid % NCV
        if kind == "dw":
            if not prologue:
                return
            j, m = args
            stg = self.UB[:, 0:CW * 128].rearrange("p (k x) -> p k x", k=CW)
            fns = [TS(stg[:, kk, :], self.identb[:], self.vrow(m, R_DW + j * CW + kk), ALU.mult) for kk in range(CW)]
            S.task("vector", fns, reads=[self.CB, self.VCB], writes=[self.stgB])
            self.mdma([DMA(dstrows[:, 0:CW * 128], self.UB[:, 0:CW * 128])], reads=[self.stgB], writes=[self.wscPB[pid]])
            return
        if prologue:
            return
        if kind == "cols":
            w2d, r0, ld, pw, segs = args
            dst = dstrows[:, 0:DC * pw].rearrange("p (c x) -> p c x", c=DC)
            fns = []
            for (c0, w, doff) in segs:
                src = bass.AP(tensor=w2d.tensor, offset=r0 * ld + c0, ap=[[ld, 128], [128 * ld, DC], [1, w]])
                fns.append(DMA(dst[:, :, doff:doff + w], src))
        elif kind == "wout":
            w2d, r0, m = args
            dst = dstrows[:, 0:FC * 128].rearrange("p (c x) -> p c x", c=FC)
            src = bass.AP(tensor=w2d.tensor, offset=r0 * D + m * 128, ap=[[D, 128], [128 * D, FC], [1, 128]])
            fns = [DMA(dst, src)]
        else:
            (i,) = args
            w2d = self.I["pe_w_proj"]
            dst = dstrows[:, 0:2 * D].rearrange("p (c x) -> p c x", c=2)
            src = bass.AP(tensor=w2d.tensor, offset=i * PED * D, ap=[[D, 128], [128 * D, 2], [1, D]])
            fns = [DMA(dst, src)]
        S.dma("gpsimd", fns, self.d_cv[k], writes=[self.wscPB[pid], self.cvchain[k]])

    def _prologue(self):
        S = self.S
        I = self.I
        self.mdma([DMA(self.identf[:], I["ident"])], writes=[self.CB])
        S.task("vector", [CP(self.identb[:], self.identf[:]), MS(self.onesm[:], 1.0 / D), MS(self.ones64[:], 1.0),
                          MS(self.PFX[0][:], 0.0), MS(self.PFX[1][:], 0.0)],
               reads=[self.CB], writes=[self.CB, self.PFXB[0], self.PFXB[1]])
        VST = self.U[0:NVROW, 0:D]
        vstB = Buf("vst")
        srcs = [("norm_g", R_NORM, 16), ("final_g", R_FINAL, 1), ("conv_b_in", R_CBIN, 4), ("conv_dw_b", R_DWB, 2),
                ("conv_norm_g", R_CNG, 2), ("conv_b_out", R_CBOUT, 2), ("conv_dw", R_DW, 2 * CW)]
        self.mdma([DMA(self.U[r0:r0 + n, 0:D], I[nm]) for nm, r0, n in srcs], writes=[vstB])
        for half in range(2):
            b = self.bank()
            S.task("tensor", [TR(self.ps[:, b * 512 + cc * 96:b * 512 + cc * 96 + NVROW],
                                 VST[:, (half * 4 + cc) * 128:(half * 4 + cc + 1) * 128], self.identf[0:NVROW, 0:NVROW])
                              for cc in range(4)], reads=[vstB, self.CB], writes=[self.PB[b]])
            S.task("vector", [CP(self.vecs[:, half * 4:half * 4 + 4, 0:NVROW],
                                 self.ps[:, b * 512:b * 512 + 384].rearrange("p (c r) -> p c r", c=4)[:, :, 0:NVROW])],
                   reads=[self.PB[b]], writes=[self.VCB])
        self.stgB = Buf("stg")
        S.task("vector", [MS(self.RS[:, 0:1], 0.0)], reads=[self.VCB], writes=[vstB, self.stgB])
        for pid in range(self.npieces):
            self.convert_piece(pid, prologue=True)
        for j in range(2):
            self._build_eb(j)

    def _build_eb(self, j):
        S = self.S
        I = self.I
        tabx = self.U[0:NH, 0:385]
        etab = self.U[0:NH, 400:785]
        rt = self.U[0:NH, 800:1185]
        onesr = self.U[0:NH, 1200:1328]
        HK = self.U[:, 2048:2048 + NH * 128].rearrange("p (h q) -> p h q", h=NH)
        tB = Buf("tabx")
        hkB = Buf("hk")
        self.mdma([DMA(tabx[:, 0:257], I["rel_table"][j * NH:(j + 1) * NH, :])], writes=[tB, self.stgB])
        S.task("vector", [MS(onesr, 1.0), TS(tabx[:, 257:385], onesr, tabx[:, 256:257], ALU.mult)],
               reads=[tB], writes=[tB])
        S.task("scalar", [ACT(etab, tabx, AF.Exp)], reads=[tB], writes=[tB])
        rev = bass.AP(tensor=self.U.tensor if hasattr(self.U, "tensor") else self.U,
                      offset=etab.offset + 384, ap=[list(etab.ap[0]), [-1, 385]])
        S.task("vector", [CP(rt, rev)], reads=[tB], writes=[tB])
        self.mdma([DMA(self.rtab[j * NH:(j + 1) * NH, :], rt)], reads=[tB], writes=[self.rtabB])
        for r in range(5):
            if r == 4:
                off, pst = 129, 1
            elif r == 3:
                off, pst = 1, 1
            else:
                off, pst = 0, 0
            src = bass.AP(tensor=self.rtab.tensor, offset=j * NH * 385 + off, ap=[[pst, 128], [385, NH], [1, 128]])
            self.mdma([DMA(HK, src)], reads=[self.rtabB], writes=[hkB])
            hkr = bass.AP(tensor=HK.tensor, offset=HK.offset + 127, ap=[list(HK.ap[0]), [128, NH], [-1, 128]])
            S.task("vector", [CP(self.EB[:, :, r, :], hkr)], reads=[hkB], writes=[self.EBB])
        S.task("vector", [MS(self.EB[0:64, :, 0, 64:128], 0.0), MS(self.EB[64:128, :, 4, 0:64], 0.0)],
               reads=[], writes=[self.EBB])
        self.mdma([DMA(self.ebsc[j * 128:(j + 1) * 128, :], self.EB[:].rearrange("p h r q -> p (h r q)"))],
                  reads=[self.EBB], writes=[self.ebscB[j]])

    def w_init(self, ntiles):
        self.wseq = self.order * ntiles
        self.wl = 0
        self.wu = 0
        for _ in range(NSLOT):
            self.w_prefetch()

    def w_prefetch(self):
        if self.wl >= len(self.wseq):
            return
        pid = self.wseq[self.wl]
        slot = self.wl % NSLOT
        plen = self.pieces[pid][2]
        while self.cv_next < min(self.npieces, self.wl + CVLOOK + 1):
            self.convert_piece(self.cv_next)
            self.cv_next += 1
        self.S.dma("sync", [DMA(self.WS[slot][:, 0:plen], self.wsc[pid * 128:(pid + 1) * 128, 0:plen])],
                   self.wdsem[slot], reads=[self.wscPB[pid]], writes=[self.WB[slot]])
        self.wl += 1

    def w_get(self, kind=None):
        pid = self.wseq[self.wu]
        if kind is not None:
            assert self.pieces[pid][0] == kind, (self.pieces[pid][0], kind)
        slot = self.wu % NSLOT
        return self.WS[slot], self.WB[slot]

    def w_done(self):
        self.wu += 1
        self.w_prefetch()

    def mmgroup(self, b, mms, reads, chunk_reads=None, split=None):
        n = len(mms)
        fns = [MM(o, l, r, i == 0, i == n - 1) for i, (o, l, r) in enumerate(mms)]
        if chunk_reads is not None and self.fresh_norm:
            self.fresh_norm = False
            common = [x for x in reads if x not in chunk_reads]
            for i, fn in enumerate(fns):
                self.S.task("tensor", [fn], reads=common + [chunk_reads[i]], writes=[self.PB[b]])
            return
        if split is not None:
            k, r1, r2 = split
            self.S.task("tensor", fns[:k], reads=r1, writes=[self.PB[b]])
            self.S.task("tensor", fns[k:], reads=r2, writes=[self.PB[b]])
            return
        self.S.task("tensor", fns, reads=reads, writes=[self.PB[b]])

    def norm(self, src, srcB, grow, N, mode, dst=None, dstB=None):
        S = self.S
        XN, XNB = self.XN, self.XNB
        self.stat_tick(drain=True)
        assert self.st_n == DC and not self.st_q, (self.st_n, len(self.st_q))
        self.st_n = 0
        ps7 = self.ps[:, 7 * 512:7 * 512 + N]
        S.task("scalar", [ACT(self.RS[:, 0:N], ps7, AF.Sqrt, bias=self.epsb[:, 0:1])],
               reads=[self.PB[7], self.CB], writes=[self.RSB])
        S.task("vector", [RCP(ps7, self.RS[:, 0:N])], reads=[self.RSB], writes=[self.PB[7]])
        for c in range(DC):
            if mode == "bf":
                o, oB = XN[:, c, 0:N], [XNB[c]]
            elif mode == "silu":
                o, oB = src[:, c, 0:N], [srcB[c]]
            else:
                o, oB = dst[:, c, 0:N], [dstB[c]]
            S.task("vector", [STT(o, src[:, c, 0:N], self.vrow(c, grow), ps7, ALU.mult, ALU.mult)],
                   reads=[srcB[c], self.PB[7], self.VCB], writes=oB)
            if mode == "silu":
                S.task("scalar", [ACT(XN[:, c, 0:N], src[:, c, 0:N], AF.Silu)], reads=[srcB[c]], writes=[XNB[c]])
        self.fresh_norm = mode != "f32"

    def ffn(self, i, k, N):
        S = self.S
        X, XB, XN, XNB = self.X, self.XB, self.XN, self.XNB
        self.norm(X, XB, R_NORM + i * 4 + (0 if k == 0 else 2), N, "bf")
        H = self.UB[:, 0:FC * T].rearrange("p (c t) -> p c t", c=FC)
        HB = self.HB
        for p in range(11):
            ws, wb = self.w_get("cols")
            w3 = ws[:, :].rearrange("p (c x) -> p c x", c=DC)
            for jj in range(2):
                jf = 2 * p + jj
                ba = self.bank()
                self.mmgroup(ba, [(self.pbank(ba, N), w3[:, c, jj * 128:(jj + 1) * 128], XN[:, c, 0:N]) for c in range(DC)],
                             reads=[wb] + XNB, chunk_reads=XNB)
                bb = self.bank()
                self.mmgroup(bb, [(self.pbank(bb, N), w3[:, c, 256 + jj * 128:256 + (jj + 1) * 128], XN[:, c, 0:N])
                                  for c in range(DC)], reads=[wb] + XNB, chunk_reads=XNB)
                si = self.sa_i
                self.sa_i ^= 1
                S.task("scalar", [ACT(self.SA[:, si, 0:N], self.pbank(ba, N), AF.Silu)], reads=[self.PB[ba]],
                       writes=[self.SAB[si]])
                S.task("vector", [TT(H[:, jf, 0:N], self.pbank(bb, N), self.SA[:, si, 0:N], ALU.mult)],
                       reads=[self.PB[bb], self.SAB[si]], writes=[HB[jf]])
            self.w_done()
        for m in range(DC):
            ws, wb = self.w_get("wout")
            w3 = ws[:, 0:FC * 128].rearrange("p (c x) -> p c x", c=FC)
            b = self.bank()
            self.mmgroup(b, [(self.pbank(b, N), w3[:, jf, :], H[:, jf, 0:N]) for jf in range(FC)], reads=[wb] + HB,
                         split=(16, [wb] + HB[:16], HB[16:]) if m == 0 else None)
            S.task("vector", [STT(X[:, m, 0:N], self.pbank(b, N), 0.5, X[:, m, 0:N], ALU.mult, ALU.add)],
                   reads=[self.PB[b], XB[m]], writes=[XB[m]])
            self.stat_tick()
            self.stat_add(X, XB, m, N)
            self.w_done()

    def convmix(self, i, N, sample, last):
        S = self.S
        j = i // 2
        X, XB, XN, XNB = self.X, self.XB, self.XN, self.XNB
        I, O = self.I, self.O
        SQ = NSEQ_S if sample else 1
        L = LS if sample else N
        PL = 30 + L
        GLB = self.UB[:, 0:DC * SQ * PL].rearrange("p (c s l) -> p c s l", c=DC, s=SQ)
        GB = self.GB
        yoff = 2240
        YC = self.U[:, yoff:yoff + DC * T].rearrange("p (c t) -> p c t", c=DC)
        YB = self.YB
        O32 = self.O32
        self.norm(X, XB, R_NORM + i * 4 + 1, N, "bf")
        if not sample:
            S.task("gpsimd", [CP(GLB[:, :, 0, 0:30], self.PFX[j][:])], reads=[self.PFXB[j]], writes=GB)
        else:
            for s in range(SQ):
                xi = s % 2
                row0 = (j * NSEQ_S + s) * 30
                S.dma("gpsimd", [DMA(self.XIN[0:30, xi, :], I["cconv"][row0:row0 + 30, :])], self.d_xin[xi],
                      writes=[self.XINB[xi]])
                for half in range(2):
                    b = self.bank()
                    S.task("tensor", [TR(self.ps[:, b * 512 + cc * 32:b * 512 + cc * 32 + 30],
                                         self.XIN[0:30, xi, (half * 4 + cc) * 128:(half * 4 + cc + 1) * 128],
                                         self.identf[0:30, 0:30]) for cc in range(4)],
                           reads=[self.XINB[xi], self.CB], writes=[self.PB[b]])
                    pv = self.ps[:, b * 512:b * 512 + 128].rearrange("p (c r) -> p c r", c=4)
                    S.task("vector", [CP(GLB[:, half * 4:half * 4 + 4, s, 0:30], pv[:, :, 0:30]),
                                      CP(O32[:, half * 4:half * 4 + 4, s, 0:14], pv[:, :, 16:30])],
                           reads=[self.PB[b]], writes=GB + [self.O32B])
        for q in range(4):
            ws, wb = self.w_get("cols")
            w3 = ws[:, :].rearrange("p (c x) -> p c x", c=DC)
            for mm in range(2):
                m = 2 * q + mm
                ba = self.bank()
                self.mmgroup(ba, [(self.pbank(ba, N), w3[:, c, mm * 128:(mm + 1) * 128], XN[:, c, 0:N]) for c in range(DC)],
                             reads=[wb] + XNB, chunk_reads=XNB)
                bg = self.bank()
                self.mmgroup(bg, [(self.pbank(bg, N), w3[:, c, 256 + mm * 128:256 + (mm + 1) * 128], XN[:, c, 0:N])
                                  for c in range(DC)], reads=[wb] + XNB, chunk_reads=XNB)
                si = self.sa_i
                self.sa_i ^= 1
                S.task("scalar", [ACT(self.SA[:, si, 0:N], self.pbank(bg, N), AF.Sigmoid,
                                      bias=self.vrow(m, R_CBIN + j * 2 + 1))],
                       reads=[self.PB[bg], self.VCB], writes=[self.SAB[si]])
                pa = self.pbank(ba, N).rearrange("p (s l) -> p s l", s=SQ)
                sa = self.SA[:, si, 0:N].rearrange("p (s l) -> p s l", s=SQ)
                fns = [STT(GLB[:, m, :, 30:30 + L], pa, self.vrow(m, R_CBIN + j * 2), sa, ALU.add, ALU.mult)]
                if sample:
                    fns.append(STT(O32[:, m, :, 14:30], pa, self.vrow(m, R_CBIN + j * 2), sa, ALU.add, ALU.mult))
                elif last:
                    fns.append(STT(O32[:, m, 0:1, 0:30], pa[:, :, L - 30:L], self.vrow(m, R_CBIN + j * 2),
                                   sa[:, :, L - 30:L], ALU.add, ALU.mult))
                S.task("vector", fns, reads=[self.PB[ba], self.SAB[si], self.VCB], writes=[GB[m], self.O32B])
            self.w_done()
        for m in range(DC):
            ws, wb = self.w_get("dw")
            w3 = ws[:, 0:CW * 128].rearrange("p (k x) -> p k x", k=CW)
            b = self.bank()
            po = self.pbank(b, N).rearrange("p (s l) -> p s l", s=SQ)
            self.mmgroup(b, [(po, w3[:, k, :], GLB[:, m, :, k:k + L]) for k in range(CW)], reads=[wb, GB[m]])
            S.task("scalar", [ACT(YC[:, m, 0:N], self.pbank(b, N), AF.Identity, bias=self.vrow(m, R_DWB + j))],
                   reads=[self.PB[b], self.VCB], writes=[YB[m]])
            self.stat_tick()
            self.stat_add(YC, YB, m, N)
            self.w_done()
        self.norm(YC, YB, R_CNG + j, N, "silu")
        for q in range(2):
            ws, wb = self.w_get("cols")
            w3 = ws[:, :].rearrange("p (c x) -> p c x", c=DC)
            for mm in range(4):
                m = 4 * q + mm
                b = self.bank()
                self.mmgroup(b, [(self.pbank(b, N), w3[:, c, mm * 128:(mm + 1) * 128], XN[:, c, 0:N]) for c in range(DC)],
                             reads=[wb] + XNB, chunk_reads=XNB)
                S.task("vector", [STT(X[:, m, 0:N], self.pbank(b, N), self.vrow(m, R_CBOUT + j), X[:, m, 0:N],
                                      ALU.add, ALU.add)], reads=[self.PB[b], XB[m], self.VCB], writes=[XB[m]])
                self.stat_tick()
                self.stat_add(X, XB, m, N)
            self.w_done()
        if not sample and not last:
            S.task("gpsimd", [CP(self.PFX[j][:], GLB[:, :, 0, L:L + 30])], reads=GB, writes=[self.PFXB[j]])
        if sample or last:
            for s in range(SQ):
                yi = self.yo_i
                self.yo_i ^= 1
                for half in range(2):
                    b = self.bank()
                    S.task("tensor", [TR(self.ps[0:30, b * 512 + cc * 128:b * 512 + (cc + 1) * 128],
                                         O32[:, half * 4 + cc, s, :], self.identf[:]) for cc in range(4)],
                           reads=[self.O32B, self.CB], writes=[self.PB[b]])
                    S.task("vector", [CP(self.YOUT[0:30, yi, half * 512:(half + 1) * 512], self.ps[0:30, b * 512:(b + 1) * 512])],
                           reads=[self.PB[b]], writes=[self.YOUTB[yi]])
                if sample:
                    row0 = (j * NSEQ_S + s) * 30
                    dst = O["convs"][row0:row0 + 30, :]
                else:
                    dst = O["convp"][j * 30:(j + 1) * 30, :]
                S.dma("gpsimd", [DMA(dst, self.YOUT[0:30, yi, :])], self.d_yout[yi], reads=[self.YOUTB[yi]],
                      writes=[self.outB])

    def attn_a(self, m, q0, nq, kblocks, oq0):
        S = self.S
        QT = self.QT
        sr = self.sr_i
        self.sr_i ^= 1
        base = sr * 3 * 512
        SR = self.ps[:, base:base + 1280].rearrange("p (h r q) -> p h r q", h=2, r=5)
        SRB = [self.PB[sr * 3 + t] for t in range(3)]
        E = self.EE[sr]
        PT = self.PP[sr]
        EB_ = self.EB
        fns = []
        for (kc0, vs, nk, r) in kblocks:
            for hh in range(2):
                fns.append(MM(SR[0:nk, hh, r, 0:nq], self.KT[hh * 64:(hh + 1) * 64, m, kc0:kc0 + nk],
                              QT[hh * 64:(hh + 1) * 64, m, q0:q0 + nq], True, True))
        S.task("tensor", fns, reads=[self.QTB[m], self.KTprevB, self.KTcurB[m]], writes=SRB)
        full = [kb for kb in kblocks if kb[2] == 128]
        part = [kb for kb in kblocks if kb[2] != 128]
        for hh in range(2):
            afn, mfn = [], []
            if full:
                r0, r1 = full[0][3], full[-1][3] + 1
                afn.append(ACT(E[:, hh, r0:r1, 0:nq], SR[:, hh, r0:r1, 0:nq], AF.Exp))
                mfn.append(TT(PT[:, hh, r0:r1, 0:nq], E[:, hh, r0:r1, 0:nq], EB_[:, 2 * m + hh, r0:r1, 0:nq], ALU.mult))
            for (kc0, vs, nk, r) in part:
                afn.append(ACT(E[0:nk, hh, r, 0:nq], SR[0:nk, hh, r, 0:nq], AF.Exp))
                mfn.append(TT(PT[0:nk, hh, r, 0:nq], E[0:nk, hh, r, 0:nq], EB_[0:nk, 2 * m + hh, r, 0:nq], ALU.mult))
            S.task("scalar", afn, reads=SRB, writes=[self.EEB[sr * 2 + hh]])
            S.task("vector", mfn, reads=[self.EEB[sr * 2 + hh], self.EBB], writes=[self.PPB[sr * 2 + hh]])
        return (m, nq, kblocks, oq0, sr)

    def attn_b(self, st):
        S = self.S
        m, nq, kblocks, oq0, sr = st
        OT = self.OT
        PT = self.PP[sr]
        ob = 6 + self.ob_i
        self.ob_i ^= 1
        OBk = self.ps[:, ob * 512:ob * 512 + 256]
        nkb = len(kblocks)
        for hh in range(2):
            fns = []
            for idx, (kc0, vs, nk, r) in enumerate(kblocks):
                fns.append(MM(OBk[hh * 64:(hh + 1) * 64, 0:nq], self.V[0:nk, vs, (2 * m + hh) * 64:(2 * m + hh + 1) * 64],
                              PT[0:nk, hh, r, 0:nq], idx == 0, idx == nkb - 1))
            for idx, (kc0, vs, nk, r) in enumerate(kblocks):
                fns.append(MM(OBk[hh * 64:(hh + 1) * 64, 128:128 + nq], self.ones64[0:nk, :],
                              PT[0:nk, hh, r, 0:nq], idx == 0, idx == nkb - 1))
            S.task("tensor", fns, reads=[self.PPB[sr * 2 + hh], self.VprevB, self.CB] + self.VcurB, writes=[self.PB[ob]])
        ri = self.rd_i
        self.rd_i ^= 1
        S.task("vector", [RCP(self.RD[:, ri, 0:nq], OBk[:, 128:128 + nq])], reads=[self.PB[ob]], writes=[self.RDB[ri]])
        S.task("vector", [TT(OT[:, m, oq0:oq0 + nq], OBk[:, 0:nq], self.RD[:, ri, 0:nq], ALU.mult)],
               reads=[self.PB[ob], self.RDB[ri]], writes=[self.OTB[m]])

    def attn_run(self, blocks):
        prev = None
        for args in blocks:
            st = self.attn_a(*args)
            if prev is not None:
                self.attn_b(prev)
            prev = st
        if prev is not None:
            self.attn_b(prev)

    def attnmix(self, i, N, sample, tile, last):
        S = self.S
        j = i // 2
        I, O = self.I, self.O
        X, XB, XN, XNB = self.X, self.XB, self.XN, self.XNB
        KT, V = self.KT, self.V
        self.QT = self.UB[:, 0:DC * T].rearrange("p (c t) -> p c t", c=DC)
        self.OT = self.UB[:, DC * T:2 * DC * T].rearrange("p (c t) -> p c t", c=DC)
        eoff = 2 * DC * T
        self.EE = [self.UB[:, eoff + s * 1280:eoff + (s + 1) * 1280].rearrange("p (h r q) -> p h r q", h=2, r=5)
                   for s in range(2)]
        self.PP = [self.UB[:, eoff + 2560 + s * 1280:eoff + 2560 + (s + 1) * 1280].rearrange("p (h r q) -> p h r q", h=2, r=5)
                   for s in range(2)]
        self.norm(X, XB, R_NORM + i * 4 + 1, N, "bf")
        for q in range(2):
            ws, wb = self.w_get("cols")
            w3 = ws[:, :].rearrange("p (c x) -> p c x", c=DC)
            for mm in range(4):
                m = 4 * q + mm
                b = self.bank()
                self.mmgroup(b, [(self.pbank(b, N), w3[:, c, mm * 128:(mm + 1) * 128], XN[:, c, 0:N]) for c in range(DC)],
                             reads=[wb] + XNB, chunk_reads=XNB)
                S.task("scalar", [ACT(self.QT[:, m, 0:N], self.pbank(b, N), AF.Copy, scale=HD ** -0.5)],
                       reads=[self.PB[b]], writes=[self.QTB[m]])
            self.w_done()
        if self.stop_at(2):
            return self.skip_w(6)
        ntb = 1 if sample else N // 128
        tbw = NS if sample else 128
        for q in range(2):
            ws, wb = self.w_get("cols")
            w3 = ws[:, :].rearrange("p (c x) -> p c x", c=DC)
            for mm in range(4):
                m = 4 * q + mm
                b = self.bank()
                self.mmgroup(b, [(self.pbank(b, N), w3[:, c, mm * 128:(mm + 1) * 128], XN[:, c, 0:N]) for c in range(DC)],
                             reads=[wb] + XNB, chunk_reads=XNB)
                S.task("scalar", [ACT(KT[:, m, T:T + N], self.pbank(b, N), AF.Copy)],
                       reads=[self.PB[b]], writes=[self.KTcurB[m]])
            if sample or last:
                for tb in range(ntb):
                    b = self.bank()
                    self.mmgroup(b, [(self.ps[0:tbw, b * 512:(b + 1) * 512], XN[:, c, tb * 128:tb * 128 + tbw], w3[:, c, :])
                                     for c in range(DC)], reads=[wb] + XNB, chunk_reads=XNB)
                    yi = self.yo_i
                    self.yo_i ^= 1
                    S.task("vector", [CP(self.YOUT[0:tbw, yi, 0:512], self.ps[0:tbw, b * 512:(b + 1) * 512])],
                           reads=[self.PB[b]], writes=[self.YOUTB[yi]])
                    if sample:
                        dst = O["ks"][j * NS:(j + 1) * NS, q * 512:(q + 1) * 512]
                    else:
                        dst = O["kp"][j * KVW + tb * 128:j * KVW + (tb + 1) * 128, q * 512:(q + 1) * 512]
                    S.dma("gpsimd", [DMA(dst, self.YOUT[0:tbw, yi, 0:512])], self.d_yout[yi], reads=[self.YOUTB[yi]],
                          writes=[self.outB])
            self.w_done()
        if self.stop_at(3):
            return self.skip_w(4)
        for q in range(2):
            ws, wb = self.w_get("cols")
            w3 = ws[:, :].rearrange("p (c x) -> p c x", c=DC)
            nvb = NSEQ_S if sample else N // 128
            vw = LS if sample else 128
            for tb in range(nvb):
                b = self.bank()
                self.mmgroup(b, [(self.ps[0:vw, b * 512:(b + 1) * 512], XN[:, c, tb * vw:(tb + 1) * vw], w3[:, c, :])
                                 for c in range(DC)], reads=[wb] + XNB, chunk_reads=XNB)
                S.task("scalar", [ACT(V[0:vw, 4 + tb, q * 512:(q + 1) * 512], self.ps[0:vw, b * 512:(b + 1) * 512], AF.Copy)],
                       reads=[self.PB[b]], writes=[self.VcurB[tb]])
                if sample or last:
                    yi = self.yo_i
                    self.yo_i ^= 1
                    S.task("vector", [CP(self.YOUT[0:vw, yi, 0:512], self.ps[0:vw, b * 512:(b + 1) * 512])],
                           reads=[self.PB[b]], writes=[self.YOUTB[yi]])
                    if sample:
                        dst = O["vs"][j * NS + tb * LS:j * NS + (tb + 1) * LS, q * 512:(q + 1) * 512]
                    else:
                        dst = O["vp"][j * KVW + tb * 128:j * KVW + (tb + 1) * 128, q * 512:(q + 1) * 512]
                    S.dma("gpsimd", [DMA(dst, self.YOUT[0:vw, yi, 0:512])], self.d_yout[yi], reads=[self.YOUTB[yi]],
                          writes=[self.outB])
            self.w_done()
        if self.stop_at(4):
            return self.skip_w(2)
        if not sample:
            blocks = []
            for qb in range(N // 128):
                kbs = []
                for r in range(5):
                    t = qb + r - 4
                    if t < 0 and tile == 0:
                        continue
                    kt = 4 + t
                    kc0 = (kt * 128) if kt < 4 else (T + (kt - 4) * 128)
                    kbs.append((kc0, kt, 128, r))
                blocks += [(m, qb * 128, 128, kbs, qb * 128) for m in range(DC)]
            self.attn_run(blocks)
        else:
            for s in range(NSEQ_S):
                self.load_sample_kv(j, s)
                kbs = [(r * 128, r, 128, r) for r in range(4)] + [(T + s * LS, 4 + s, LS, 4)]
                self.attn_run([(m, s * LS, LS, kbs, s * LS) for m in range(DC)])
        if self.stop_at(5):
            return self.skip_w(2)
        for q in range(2):
            ws, wb = self.w_get("cols")
            w3 = ws[:, :].rearrange("p (c x) -> p c x", c=DC)
            for mm in range(4):
                m = 4 * q + mm
                b = self.bank()
                self.mmgroup(b, [(self.pbank(b, N), w3[:, c, mm * 128:(mm + 1) * 128], self.OT[:, c, 0:N]) for c in range(DC)],
                             reads=[wb] + self.OTB)
                S.task("vector", [TT(X[:, m, 0:N], self.pbank(b, N), X[:, m, 0:N], ALU.add)],
                       reads=[self.PB[b], XB[m]], writes=[XB[m]])
                self.stat_tick()
                self.stat_add(X, XB, m, N)
            self.w_done()
        if self.stop_at(6):
            return
        if not sample and not last:
            S.dma("gpsimd", [DMA(self.kvsc[(2 * j) * 128:(2 * j + 1) * 128, :].rearrange("p (c t) -> p c t", c=DC),
                                 KT[:, :, T:2 * T]),
                             DMA(self.kvsc[(2 * j + 1) * 128:(2 * j + 2) * 128, :].rearrange("p (k d) -> p k d", k=4),
                                 V[:, 4:8, :])],
                  self.d_kvs, reads=self.KTcurB + self.VcurB, writes=[self.kvscB[2 * j], self.kvscB[2 * j + 1]])

    def skip_w(self, n):
        for _ in range(n):
            self.w_get()
            self.w_done()

    def load_prev_kv(self, j):
        S = self.S
        S.dma("gpsimd", [DMA(self.KT[:, :, 0:T], self.kvsc[(2 * j) * 128:(2 * j + 1) * 128, :].rearrange("p (c t) -> p c t", c=DC)),
                         DMA(self.V[:, 0:4, :], self.kvsc[(2 * j + 1) * 128:(2 * j + 2) * 128, :].rearrange("p (k d) -> p k d", k=4))],
              self.d_kvl, reads=[self.kvscB[2 * j], self.kvscB[2 * j + 1]], writes=[self.KTprevB, self.VprevB])

    def load_eb(self, j):
        self.S.dma("gpsimd", [DMA(self.EB[:].rearrange("p h r q -> p (h r q)"), self.ebsc[j * 128:(j + 1) * 128, :])],
                   self.d_eb, reads=[self.ebscB[j]], writes=[self.EBB])

    def load_sample_kv(self, j, s):
        S = self.S
        I = self.I
        row0 = (j * NSEQ_S + s) * KVW
        S.dma("gpsimd", [DMA(self.V[:, 0:4, :], I["cv"][row0:row0 + KVW, :].rearrange("(k p) d -> p k d", p=128))],
              self.d_kvl, writes=[self.VprevB])
        S.dma("gpsimd", [DMA(self.KST[:], I["ck"][row0:row0 + KVW, :].rearrange("(k p) d -> p k d", p=128))],
              self.d_kst, writes=[self.KSTB])
        for c in range(DC):
            b = self.bank()
            S.task("tensor", [TR(self.psb[:, b * 1024 + kt * 128:b * 1024 + (kt + 1) * 128],
                                 self.KST[:, kt, c * 128:(c + 1) * 128], self.identb[:]) for kt in range(4)],
                   reads=[self.KSTB, self.CB], writes=[self.PB[b]])
            S.task("vector", [CP(self.KT[:, c, 0:T], self.psb[:, b * 1024:b * 1024 + 512])],
                   reads=[self.PB[b]], writes=[self.KTprevB])

    def load_p(self, i, sample, tile):
        I = self.I
        if sample:
            self.S.dma("gpsimd", [DMA(self.PIN[0:NS, 0, :], I["psm"][i * NS:(i + 1) * NS, :])], self.d_pin,
                       writes=[self.PINB])
        else:
            r0 = i * self.SEQ + tile * T
            self.S.dma("gpsimd", [DMA(self.PIN[:], I["pp"][r0:r0 + T, :].rearrange("(b p) d -> p b d", p=128))],
                       self.d_pin, writes=[self.PINB])

    def pegate(self, i, N, sample):
        S = self.S
        X, XB, XN, XNB = self.X, self.XB, self.XN, self.XNB
        nb = 1 if sample else N // 128
        bw = NS if sample else 128
        self.norm(X, XB, R_NORM + i * 4 + 3, N, "bf")
        for half in range(2):
            b = self.bank()
            S.task("tensor", [TR(self.ps[:, b * 512 + tb * 128:b * 512 + tb * 128 + bw],
                                 self.PIN[0:bw, tb, half * 128:(half + 1) * 128], self.identf[0:bw, 0:bw])
                              for tb in range(nb)], reads=[self.PINB, self.CB], writes=[self.PB[b]])
            S.task("scalar", [ACT(self.PTT[:, half, 0:N], self.pbank(b, N), AF.Copy)], reads=[self.PB[b]],
                   writes=[self.PTTB])
        PPJ = self.U[:, 0:DC * T].rearrange("p (c t) -> p c t", c=DC)
        TMP = self.U[:, DC * T:DC * T + 2 * T].rearrange("p (s t) -> p s t", s=2)
        ws, wb = self.w_get("proj")
        w3 = ws[:, 0:2 * D].rearrange("p (c x) -> p c x", c=2)
        for m in range(DC):
            b = self.bank()
            self.mmgroup(b, [(self.pbank(b, N), w3[:, cc, m * 128:(m + 1) * 128], self.PTT[:, cc, 0:N]) for cc in range(2)],
                         reads=[wb, self.PTTB])
            S.task("scalar", [ACT(PPJ[:, m, 0:N], self.pbank(b, N), AF.Copy)], reads=[self.PB[b]], writes=[self.PPJB[m]])
        self.w_done()
        for q in range(2):
            ws, wb = self.w_get("cols")
            w3 = ws[:, :].rearrange("p (c x) -> p c x", c=DC)
            for mm in range(4):
                m = 4 * q + mm
                b = self.bank()
                self.mmgroup(b, [(self.pbank(b, N), w3[:, c, mm * 128:(mm + 1) * 128], XN[:, c, 0:N]) for c in range(DC)],
                             reads=[wb] + XNB, chunk_reads=XNB)
                si = self.sa_i
                self.sa_i ^= 1
                S.task("scalar", [ACT(self.SA[:, si, 0:N], self.pbank(b, N), AF.Sigmoid)], reads=[self.PB[b]],
                       writes=[self.SAB[si]])
                S.task("vector", [TT(TMP[:, si, 0:N], self.SA[:, si, 0:N], PPJ[:, m, 0:N], ALU.mult)],
                       reads=[self.SAB[si], self.PPJB[m]], writes=[self.TMPB[si]])
                S.task("gpsimd", [TT(X[:, m, 0:N], X[:, m, 0:N], TMP[:, si, 0:N], ALU.add)],
                       reads=[self.TMPB[si], XB[m]], writes=[XB[m]])
                self.stat_tick()
                self.stat_add(X, XB, m, N)
            self.w_done()

    def load_x(self, sample, tile):
        S = self.S
        I = self.I
        nb = 1 if sample else 4
        bw = NS if sample else 128
        for tb in range(nb):
            xi = tb % 2
            src = I["xs"] if sample else I["xp"][tile * T + tb * 128:tile * T + (tb + 1) * 128, :]
            S.dma("gpsimd", [DMA(self.XIN[0:bw, xi, :], src)], self.d_xin[xi], writes=[self.XINB[xi]])
            for half in range(2):
                b = self.bank()
                S.task("tensor", [TR(self.ps[:, b * 512 + cc * 128:b * 512 + cc * 128 + bw],
                                     self.XIN[0:bw, xi, (half * 4 + cc) * 128:(half * 4 + cc + 1) * 128],
                                     self.identf[0:bw, 0:bw]) for cc in range(4)],
                       reads=[self.XINB[xi], self.CB], writes=[self.PB[b]])
                pv = self.ps[:, b * 512:(b + 1) * 512].rearrange("p (c t) -> p c t", c=4)
                S.task("vector", [CP(self.X[:, half * 4:half * 4 + 4, tb * 128:tb * 128 + bw], pv[:, :, 0:bw])],
                       reads=[self.PB[b]], writes=self.XB[half * 4:half * 4 + 4])
        for c in range(DC):
            self.stat_add(self.X, self.XB, c, NS if sample else T)

    def store_y(self, N, sample, tile):
        S = self.S
        O = self.O
        YF = self.U[:, 0:DC * T].rearrange("p (c t) -> p c t", c=DC)
        self.norm(self.X, self.XB, R_FINAL, N, "f32", dst=YF, dstB=self.YFB)
        nb = 1 if sample else 4
        bw = NS if sample else 128
        for tb in range(nb):
            yi = self.yo_i
            self.yo_i ^= 1
            for half in range(2):
                b = self.bank()
                S.task("tensor", [TR(self.ps[0:bw, b * 512 + cc * 128:b * 512 + (cc + 1) * 128],
                                     YF[:, half * 4 + cc, tb * 128:tb * 128 + bw], self.identf[:]) for cc in range(4)],
                       reads=self.YFB + [self.CB], writes=[self.PB[b]])
                S.task("scalar", [ACT(self.YOUT[0:bw, yi, half * 512:(half + 1) * 512], self.ps[0:bw, b * 512:(b + 1) * 512],
                                      AF.Copy)], reads=[self.PB[b]], writes=[self.YOUTB[yi]])
            dst = O["ys"] if sample else O["yp"][tile * T + tb * 128:tile * T + (tb + 1) * 128, :]
            S.dma("gpsimd", [DMA(dst, self.YOUT[0:bw, yi, :])], self.d_yout[yi], reads=[self.YOUTB[yi]], writes=[self.outB])

    def _main(self):
        S = self.S
        self.epsb = self.sb("epsb", [128, 1], F32)
        S.task("vector", [MS(self.epsb[:], EPS)], writes=[self.CB])
        self.HB = [Buf("H%d" % f) for f in range(FC)]
        self.GB = [Buf("G%d" % c) for c in range(DC)]
        self.YB = [Buf("YC%d" % c) for c in range(DC)]
        self.QTB = [Buf("QT%d" % c) for c in range(DC)]
        self.OTB = [Buf("OT%d" % c) for c in range(DC)]
        self.EEB = [Buf("EE%d" % t) for t in range(4)]
        self.PPB = [Buf("PP%d" % t) for t in range(4)]
        self.PPJB = [Buf("PPJ%d" % c) for c in range(DC)]
        self.TMPB = [Buf("TMP0"), Buf("TMP1")]
        self.YFB = [Buf("YF%d" % c) for c in range(DC)]
        self.outB = Buf("out")
        self.sr_i = 0
        self.ob_i = 0
        tiles = [(False, t) for t in range(self.NT)] + [(True, 0)]
        self.w_init(len(tiles))
        import os
        dbg = os.environ.get("KDBG")
        dbg = tuple(int(v) for v in dbg.split(",")) if dbg else None
        self.dbgp = None
        for ti, (sample, tile) in enumerate(tiles):
            if dbg is not None and (ti > dbg[0] or (ti == dbg[0] and dbg[1] == 0)):
                break
            N = NS if sample else T
            last = (not sample) and tile == self.NT - 1
            self.load_x(sample, tile)
            for i in range(DEPTH):
                j = i // 2
                if dbg is not None and ti == dbg[0] and i >= dbg[1]:
                    break
                self.dbgp = (dbg[2],) if (dbg is not None and len(dbg) > 2 and ti == dbg[0] and i == dbg[1] - 1) else None
                self.load_p(i, sample, tile)
                if i % 2 == 1:
                    self.load_eb(j)
                    if not sample and tile > 0:
                        self.load_prev_kv(j)
                self.uphase()
                self.ffn(i, 0, N)
                self.uphase()
                if i % 2 == 0:
                    self.convmix(i, N, sample, last)
                else:
                    self.attnmix(i, N, sample, tile, last)
                self.uphase()
                self.ffn(i, 1, N)
                self.uphase()
                self.pegate(i, N, sample)
            if dbg is not None and ti == dbg[0]:
                break
            self.uphase()
            self.store_y(N, sample, tile)
        assert dbg is not None or self.wu == len(self.wseq), (self.wu, len(self.wseq))

    def stop_at(self, k):
        d = self.dbgp
        return d is not None and d[0] == k

    def uphase(self):
        allb = (self.HB + self.GB + self.YB + self.QTB + self.OTB + self.EEB + self.PPB + self.PPJB + self.TMPB + self.YFB)
        toks = {}
        for b in allb:
            for t in ([b.w] if b.w else []) + b.r:
                if toks.get(t[0], 0) < t[1]:
                    toks[t[0]] = t[1]
        for b in allb:
            b.w = None
            b.r = list(toks.items())


_NC_CACHE = {}


def _get_nc(SEQ):
    if SEQ not in _NC_CACHE:
        _NC_CACHE[SEQ] = Builder(SEQ).build()
    return _NC_CACHE[SEQ]


def kernel(x_prompt, x_sample, cache_conv, cache_k, cache_v, p_prompt, p_sample,
           norm_g, final_g, ffn_w_in, ffn_w_out, conv_w_in, conv_b_in, conv_dw, conv_dw_b,
           conv_norm_g, conv_w_out, conv_b_out, attn_w_qkv, attn_w_o, attn_rel_table,
           pe_w_proj, pe_w_gate):
    f = lambda a: np.ascontiguousarray(np.asarray(a, dtype=np.float32))
    B, SEQ, _ = x_prompt.shape
    NCORE = 8
    assert B == NCORE and x_sample.shape[0] == NCORE * NSEQ_S and x_sample.shape[1] == LS
    nc = _get_nc(SEQ)
    shared = {
        "norm_g": f(norm_g).reshape(16, D), "final_g": f(final_g).reshape(1, D),
        "ffn_w_in": f(ffn_w_in).reshape(8 * D, 2 * DFF), "ffn_w_out": f(ffn_w_out).reshape(8 * DFF, D),
        "conv_w_in": f(conv_w_in).reshape(2 * D, 2 * D), "conv_b_in": f(conv_b_in).reshape(4, D),
        "conv_dw": f(conv_dw).reshape(2 * CW, D), "conv_dw_b": f(conv_dw_b).reshape(2, D),
        "conv_norm_g": f(conv_norm_g).reshape(2, D), "conv_w_out": f(conv_w_out).reshape(2 * D, D),
        "conv_b_out": f(conv_b_out).reshape(2, D), "attn_w_qkv": f(attn_w_qkv).reshape(2 * D, 3 * D),
        "attn_w_o": f(attn_w_o).reshape(2 * D, D), "rel_table": f(attn_rel_table).reshape(2 * NH, 257),
        "pe_w_proj": f(pe_w_proj).reshape(DEPTH * PED, D), "pe_w_gate": f(pe_w_gate).reshape(DEPTH * D, D),
        "ident": np.eye(128, dtype=np.float32),
    }
    xp, xs = f(x_prompt), f(x_sample)
    cc, ck, cv = f(cache_conv), f(cache_k), f(cache_v)
    pp, psm = f(p_prompt), f(p_sample)
    in_maps = []
    for c in range(NCORE):
        sl = slice(c * NSEQ_S, (c + 1) * NSEQ_S)
        m = dict(shared)
        m["xp"] = xp[c]
        m["xs"] = xs[sl].reshape(NS, D)
        m["cconv"] = cc[:, sl].reshape(2 * NSEQ_S * 30, D)
        m["ck"] = ck[:, sl].reshape(2 * NSEQ_S * KVW, D)
        m["cv"] = cv[:, sl].reshape(2 * NSEQ_S * KVW, D)
        m["pp"] = pp[:, c].reshape(DEPTH * SEQ, PED)
        m["psm"] = psm[:, sl].reshape(DEPTH * NS, PED)
        in_maps.append(m)
    res = run_bass_kernel_spmd(nc, in_maps, core_ids=list(range(NCORE)))
    R = res.results
    y_prompt = np.stack([R[c]["yp"] for c in range(NCORE)], 0)
    y_sample = np.concatenate([R[c]["ys"].reshape(NSEQ_S, LS, D) for c in range(NCORE)], 0)
    conv_prompt = np.stack([R[c]["convp"].reshape(2, 30, D) for c in range(NCORE)], 1)
    k_prompt = np.stack([R[c]["kp"].reshape(2, KVW, NH, HD) for c in range(NCORE)], 1)
    v_prompt = np.stack([R[c]["vp"].reshape(2, KVW, NH, HD) for c in range(NCORE)], 1)
    conv_sample = np.concatenate([R[c]["convs"].reshape(2, NSEQ_S, 30, D) for c in range(NCORE)], 1)
    k_sample = np.concatenate([R[c]["ks"].reshape(2, NSEQ_S, LS, NH, HD) for c in range(NCORE)], 1)
    v_sample = np.concatenate([R[c]["vs"].reshape(2, NSEQ_S, LS, NH, HD) for c in range(NCORE)], 1)
    return (y_prompt, y_sample, conv_prompt, k_prompt, v_prompt, conv_sample, k_sample, v_sample)
```
